# Optimizing a Trainium2 kernel written in Bass

```python
import functools
import jax
import jax.numpy as jnp
from jax import lax
import numpy as np

D_MODEL = 4096
BATCH = 8
SEQ = 2048
DEPTH = 2

CTX_LEN = 256
GRID_W = 64

MIX_W = D_MODEL
GROUP_W = MIX_W // 4

A_HD = 64
A_HEADS = GROUP_W // A_HD
A_W = A_HEADS * A_HD
A_DECAY_RANK = 64
A_ICLR_RANK = 64
A_GATE_RANK = 160
A_COLS = 3 * A_W + 2 * A_DECAY_RANK + 2 * A_ICLR_RANK + A_GATE_RANK
RWKV_DECAY_SCALE = 0.6065306597
RWKV_GN_EPS = 64e-5

B_HEADS = 4
B_V = GROUP_W // B_HEADS
B_QK = B_V // 2
B_W = B_HEADS * B_V
B_CHUNK = 128
B_COLS = 2 * B_HEADS * B_QK + 2 * B_W

C_HD = 64
C_HEADS = GROUP_W // C_HD
C_W = C_HEADS * C_HD
WIN_R = 8
WIN_C = 16
C_COLS = 3 * C_W

D_HEADS = 4
D_V = GROUP_W // D_HEADS
D_QK = D_V // 2
D_W = D_HEADS * D_V
D_RANK = 16
D_TAU = 16.0
D_CHUNK = 64
D_COLS = 2 * D_HEADS * D_QK + 2 * D_W + 2 * D_RANK

IN_COLS = A_COLS + B_COLS + C_COLS + D_COLS

N_EXPERTS = 16
EXPERT_FF = D_MODEL // 4
EC_FACTOR = 2

ROPE_BASE = 10000.0
LN_EPS = 1e-6
GN_EPS = 1e-5
DN_ALPHA = (2 * DEPTH) ** 0.25
DN_BETA = (8 * DEPTH) ** -0.25

kernel_name = 'hybrid_diffusion_block'


def split_cols(z, sizes):
    out, start = [], 0
    for s in sizes:
        out.append(z[..., start:start + s])
        start += s
    return out


def to_heads(t, n_heads):
    return t.reshape(t.shape[:-1] + (n_heads, t.shape[-1] // n_heads))


def layer_norm(x, eps=LN_EPS):
    xf = x.astype(jnp.float32)
    mu = jnp.mean(xf, axis=-1, keepdims=True)
    var = jnp.mean(jnp.square(xf - mu), axis=-1, keepdims=True)
    return ((xf - mu) * lax.rsqrt(var + eps)).astype(x.dtype)


def group_norm(y, n_heads, gain, bias, eps):
    return layer_norm(to_heads(y, n_heads), eps).reshape(y.shape) * gain + bias


def modulate(x, shift, scale):
    return layer_norm(x) * (1.0 + scale) + shift


def post_norm(x, y, ln):
    return layer_norm(DN_ALPHA * x + y) * ln[0] + ln[1]


def centred_shift(z, mu):
    z_prev = jnp.pad(z, ((0, 0), (1, 0), (0, 0)))[:, :-1]
    z_next = jnp.pad(z, ((0, 0), (0, 1), (0, 0)))[:, 1:]
    return z + mu[0] * (z_prev - z) + mu[1] * (z_next - z)


def axial_rope(t):
    T, dh = t.shape[1], t.shape[-1]
    half = dh // 2
    pos = jnp.arange(T)
    inv_freq = ROPE_BASE ** (-jnp.arange(0, half, 2, dtype=jnp.float32) / half)

    def rotate(u, p):
        ang = p.astype(jnp.float32)[:, None] * inv_freq[None, :]
        cos, sin = jnp.cos(ang)[None, :, None, :], jnp.sin(ang)[None, :, None, :]
        u1, u2 = u[..., :half // 2], u[..., half // 2:]
        return jnp.concatenate([u1 * cos - u2 * sin, u1 * sin + u2 * cos], axis=-1)

    out = jnp.concatenate([rotate(t[..., :half], pos // GRID_W), rotate(t[..., half:], pos % GRID_W)], axis=-1)
    return out.astype(t.dtype)


def bidirectional(fns, xs_dirs, s0s, emit):
    ys, states = [], []
    for d in range(2):
        xs = xs_dirs[d] if d == 0 else tuple(jnp.flip(t, axis=1) for t in xs_dirs[d])
        y, s = fns[d](*xs, s0s[d], emit)
        ys.append(jnp.flip(y, axis=1) if (emit and d == 1) else y)
        states.append(s)
    return (ys[0] + ys[1] if emit else None), states


def context_then_latent(fns, xs_ctx, xs_lat, s0s, emit_ctx):
    y_ctx, s_ctx = bidirectional(fns, xs_ctx, s0s, emit_ctx)
    y_lat, _ = bidirectional(fns, xs_lat, s_ctx, True)
    return y_ctx, y_lat


def rwkv7_scan(r, w, k, v, kk, a, s0, emit):
    def step(s, xs):
        r_t, w_t, k_t, v_t, kk_t, a_t = xs
        sa = jnp.einsum('bhvk,bhk->bhv', s, kk_t)
        s = (s * w_t[:, :, None, :] - sa[..., None] * (kk_t * a_t)[:, :, None, :]
             + v_t[..., None] * k_t[:, :, None, :])
        return s, (jnp.einsum('bhvk,bhk->bhv', s, r_t) if emit else None)

    s_fin, ys = lax.scan(step, s0, tuple(jnp.moveaxis(t, 1, 0) for t in (r, w, k, v, kk, a)))
    return (jnp.moveaxis(ys, 0, 1) if emit else None), s_fin


def rwkv7_mixer(z_ctx, z_lat, mu, w0, w2, a0, a2, g2, k_k, k_a, r_k, gn, emit_ctx):
    def features(z):
        z = centred_shift(z, mu)
        B, T, _ = z.shape
        r, k, v, wc, ac, gc = split_cols(z, (A_W, A_W, A_W, 2 * A_DECAY_RANK, 2 * A_ICLR_RANK, A_GATE_RANK))
        wc = wc.reshape(B, T, 2, A_DECAY_RANK)
        ac = ac.reshape(B, T, 2, A_ICLR_RANK)
        w = jnp.exp(-RWKV_DECAY_SCALE * jax.nn.sigmoid(w0 + jnp.einsum('btdr,drc->btdc', jnp.tanh(wc), w2)))
        a = jax.nn.sigmoid(a0 + jnp.einsum('btdr,drc->btdc', ac, a2))
        kk = to_heads(k * k_k, A_HEADS)
        kk = kk * lax.rsqrt(jnp.sum(jnp.square(kk), axis=-1, keepdims=True) + 1e-12)
        k_dir = k[:, :, None, :] * (1.0 + (a - 1.0) * k_a)
        return (to_heads(r, A_HEADS), to_heads(w, A_HEADS), to_heads(k_dir, A_HEADS),
                to_heads(v, A_HEADS), kk, to_heads(a, A_HEADS), gc)

    def per_dir(f):
        r, w, kd, v, kk, a, _ = f
        return [(r, w[:, :, d], kd[:, :, d], v, kk, a[:, :, d]) for d in range(2)]

    def readout(f, y):
        r, _, kd, v, _, _, gc = f
        B, T = r.shape[:2]
        bonus = jnp.sum(r * kd[:, :, 0] * r_k, axis=-1, keepdims=True) * v
        g = jax.nn.sigmoid(gc) @ g2
        return (group_norm(y.reshape(B, T, A_W), A_HEADS, gn[0], gn[1], RWKV_GN_EPS)
                + bonus.reshape(B, T, A_W)) * g

    f_ctx, f_lat = features(z_ctx), features(z_lat)
    s0 = jnp.zeros((z_lat.shape[0], A_HEADS, A_HD, A_HD), jnp.float32)
    y_ctx, y_lat = context_then_latent((rwkv7_scan, rwkv7_scan), per_dir(f_ctx), per_dir(f_lat), (s0, s0), emit_ctx)
    return (readout(f_ctx, y_ctx) if emit_ctx else None), readout(f_lat, y_lat)


def chunk_retention(log_g, q, k, v, s0, emit):
    B, T, H, _ = q.shape
    dv = v.shape[-1]
    L = B_CHUNK
    n = T // L
    qc, kc, vc = (t.reshape(B, n, L, H, t.shape[-1]) for t in (q, k, v))
    pos = jnp.arange(L, dtype=jnp.float32)
    diff = pos[:, None] - pos[None, :]
    intra_decay = jnp.where(diff >= 0, jnp.exp(log_g[:, None, None] * jnp.maximum(diff, 0.0)), 0.0)
    in_decay = jnp.exp(log_g[:, None] * (pos + 1.0))
    out_decay = jnp.exp(log_g[:, None] * (L - 1.0 - pos))
    chunk_decay = jnp.exp(log_g * L)
    kv = jnp.einsum('bnlhk,hl,bnlhv->bnhkv', kc, out_decay, vc)

    def step(s, kv_n):
        return s * chunk_decay[:, None, None] + kv_n, (s if emit else None)

    s_fin, s_prev = lax.scan(step, s0, jnp.moveaxis(kv, 1, 0))
    if not emit:
        return None, s_fin
    s_prev = jnp.moveaxis(s_prev, 0, 1)
    scores = jnp.einsum('bnihk,bnjhk->bnhij', qc, kc) * intra_decay
    y = (jnp.einsum('bnhij,bnjhv->bnihv', scores, vc)
         + jnp.einsum('bnihk,bnhkv,hi->bnihv', qc, s_prev, in_decay))
    return y.reshape(B, T, H, dv), s_fin


def retention_mixer(z_ctx, z_lat, gn, emit_ctx):
    lg_fwd = jnp.log1p(-jnp.exp2(-5.0 - jnp.arange(B_HEADS, dtype=jnp.float32)))
    lg_bwd = lg_fwd[::-1]

    def features(z, rotary):
        q, k, v, g = split_cols(z, (B_HEADS * B_QK, B_HEADS * B_QK, B_W, B_W))
        q, k = to_heads(q, B_HEADS), to_heads(k, B_HEADS) * (B_QK ** -0.5)
        if rotary:
            q, k = axial_rope(q), axial_rope(k)
        return (q, k, to_heads(v, B_HEADS)), g

    def readout(y, g):
        B, T = g.shape[:2]
        return group_norm(y.reshape(B, T, B_W), B_HEADS, gn[0], gn[1], GN_EPS) * jax.nn.silu(g)

    xs_c, g_c = features(z_ctx, False)
    xs_l, g_l = features(z_lat, True)
    fns = (functools.partial(chunk_retention, lg_fwd), functools.partial(chunk_retention, lg_bwd))
    s0 = jnp.zeros((z_lat.shape[0], B_HEADS, B_QK, B_V), jnp.float32)
    y_ctx, y_lat = context_then_latent(fns, (xs_c, xs_c), (xs_l, xs_l), (s0, s0), emit_ctx)
    return (readout(y_ctx, g_c) if emit_ctx else None), readout(y_lat, g_l)


def neighbourhood_attention(q, k, v, k_ctx, v_ctx, rpb):
    B, T, H, dh = q.shape
    rows = T // GRID_W
    kr = min(WIN_R, rows)
    nk = kr * WIN_C
    scale = dh ** -0.5
    r = jnp.arange(rows)
    col = jnp.arange(GRID_W)
    key_rows = jnp.clip(r - kr // 2, 0, rows - kr)[:, None] + jnp.arange(kr)[None, :]
    key_cols = jnp.clip(col - WIN_C // 2, 0, GRID_W - WIN_C)[:, None] + jnp.arange(WIN_C)[None, :]
    idx = (key_rows[:, None, :, None] * GRID_W + key_cols[None, :, None, :]).reshape(rows, GRID_W, nk)
    d_row = key_rows - r[:, None] + (WIN_R - 1)
    d_col = key_cols - col[:, None] + (WIN_C - 1)
    bias = rpb[:, d_row[:, None, :, None], d_col[None, :, None, :]].reshape(H, rows, GRID_W, nk)
    bias = jnp.moveaxis(bias, 1, 0)
    q_rows = jnp.transpose(q.reshape(B, rows, GRID_W, H, dh), (1, 0, 3, 2, 4))
    k_h, v_h = jnp.swapaxes(k, 1, 2), jnp.swapaxes(v, 1, 2)
    kc_h, vc_h = jnp.swapaxes(k_ctx, 1, 2), jnp.swapaxes(v_ctx, 1, 2)

    def row_block(args):
        q_r, idx_r, bias_r = args
        k_g, v_g = k_h[:, :, idx_r], v_h[:, :, idx_r]
        s_loc = jnp.einsum('bhqd,bhqkd->bhqk', q_r, k_g) * scale + bias_r
        s_ctx = jnp.einsum('bhqd,bhcd->bhqc', q_r, kc_h) * scale
        p = jax.nn.softmax(jnp.concatenate([s_loc, s_ctx], axis=-1).astype(jnp.float32), axis=-1).astype(v.dtype)
        return (jnp.einsum('bhqk,bhqkd->bhqd', p[..., :nk], v_g)
                + jnp.einsum('bhqc,bhcd->bhqd', p[..., nk:], vc_h))

    o = lax.map(row_block, (q_rows, idx, bias))
    return jnp.transpose(o, (1, 0, 3, 2, 4)).reshape(B, T, H * dh)


def context_attention(q, k, v):
    B, Lc, H, dh = q.shape
    s = jnp.einsum('bqhd,bkhd->bhqk', q, k) * (dh ** -0.5)
    p = jax.nn.softmax(s.astype(jnp.float32), axis=-1).astype(v.dtype)
    return jnp.einsum('bhqk,bkhd->bqhd', p, v).reshape(B, Lc, H * dh)


def natten_mixer(z_ctx, z_lat, rpb, emit_ctx):
    q_c, k_c, v_c = (to_heads(t, C_HEADS) for t in split_cols(z_ctx, (C_W, C_W, C_W)))
    q_l, k_l, v_l = (to_heads(t, C_HEADS) for t in split_cols(z_lat, (C_W, C_W, C_W)))
    y_lat = neighbourhood_attention(q_l, k_l, v_l, k_c, v_c, rpb)
    y_ctx = context_attention(q_c, k_c, v_c) if emit_ctx else None
    return y_ctx, y_lat


def chunk_gla(q, k, v, log_a, s0, emit):
    B, T, H, _ = q.shape
    dv = v.shape[-1]
    L = D_CHUNK
    n = T // L
    qc, kc, vc, lac = (t.reshape(B, n, L, H, t.shape[-1]) for t in (q, k, v, log_a))
    b = jnp.cumsum(lac, axis=2)
    b_last = b[:, :, -1]
    kv = jnp.einsum('bnlhk,bnlhv->bnhkv', kc * jnp.exp(b_last[:, :, None] - b), vc)

    def step(s, xs):
        dec, kv_n = xs
        return s * dec[..., None] + kv_n, (s if emit else None)

    s_fin, s_prev = lax.scan(step, s0, (jnp.moveaxis(jnp.exp(b_last), 1, 0), jnp.moveaxis(kv, 1, 0)))
    if not emit:
        return None, s_fin
    s_prev = jnp.moveaxis(s_prev, 0, 1)
    q_dec = qc * jnp.exp(b)
    scores = jnp.einsum('bnihk,bnjhk->bnhij', q_dec, kc * jnp.exp(-b))
    scores = jnp.where(jnp.tril(jnp.ones((L, L), dtype=bool)), scores, 0.0)
    y = (jnp.einsum('bnhij,bnjhv->bnihv', scores, vc)
         + jnp.einsum('bnihk,bnhkv->bnihv', q_dec, s_prev))
    return y.reshape(B, T, H, dv), s_fin


def gla_mixer(z_ctx, z_lat, a2, ab, gn, emit_ctx):
    def features(z):
        B, T, _ = z.shape
        q, k, v, g, ac = split_cols(z, (D_HEADS * D_QK, D_HEADS * D_QK, D_W, D_W, 2 * D_RANK))
        ac = ac.reshape(B, T, 2, D_RANK)
        log_a = jax.nn.log_sigmoid((jnp.einsum('btdr,drc->btdc', ac, a2) + ab).astype(jnp.float32)) / D_TAU
        q = to_heads(q, D_HEADS) * (D_QK ** -0.5)
        k, v, log_a = to_heads(k, D_HEADS), to_heads(v, D_HEADS), to_heads(log_a, D_HEADS)
        return [(q, k, v, log_a[:, :, d]) for d in range(2)], g

    def readout(y, g):
        B, T = g.shape[:2]
        return group_norm(y.reshape(B, T, D_W), D_HEADS, gn[0], gn[1], GN_EPS) * jax.nn.silu(g)

    xs_c, g_c = features(z_ctx)
    xs_l, g_l = features(z_lat)
    s0 = jnp.zeros((z_lat.shape[0], D_HEADS, D_QK, D_V), jnp.float32)
    y_ctx, y_lat = context_then_latent((chunk_gla, chunk_gla), xs_c, xs_l, (s0, s0), emit_ctx)
    return (readout(y_ctx, g_c) if emit_ctx else None), readout(y_lat, g_l)


def token_mixer(h_ctx, h_lat, w_in, w_out, a_mu, a_w0, a_w2, a_a0, a_a2, a_g2, a_kk, a_ka, a_rk,
                a_gn, b_gn, c_rpb, d_a2, d_ab, d_gn, emit_ctx):
    sizes = (A_COLS, B_COLS, C_COLS, D_COLS)
    za_c, zb_c, zc_c, zd_c = split_cols(h_ctx @ w_in, sizes)
    za_l, zb_l, zc_l, zd_l = split_cols(h_lat @ w_in, sizes)
    ya = rwkv7_mixer(za_c, za_l, a_mu, a_w0, a_w2, a_a0, a_a2, a_g2, a_kk, a_ka, a_rk, a_gn, emit_ctx)
    yb = retention_mixer(zb_c, zb_l, b_gn, emit_ctx)
    yc = natten_mixer(zc_c, zc_l, c_rpb, emit_ctx)
    yd = gla_mixer(zd_c, zd_l, d_a2, d_ab, d_gn, emit_ctx)

    def merge(ys, dt):
        return jnp.concatenate([y.astype(dt) for y in ys], axis=-1) @ w_out

    y_lat = merge([ya[1], yb[1], yc[1], yd[1]], h_lat.dtype)
    y_ctx = merge([ya[0], yb[0], yc[0], yd[0]], h_ctx.dtype) if emit_ctx else None
    return y_ctx, y_lat


def expert_choice_ffn(h, w_router, w1, w3, w2):
    B, T, D = h.shape
    cap = EC_FACTOR * T // N_EXPERTS
    aff = jax.nn.softmax((h @ w_router).astype(jnp.float32), axis=-1)
    gate, idx = lax.top_k(jnp.swapaxes(aff, 1, 2), cap)
    xs = jax.vmap(lambda hb, ib: hb[ib])(h, idx)
    hid = jax.nn.silu(jnp.einsum('becd,edf->becf', xs, w1)) * jnp.einsum('becd,edf->becf', xs, w3)
    y = jnp.einsum('becf,efd->becd', hid, w2) * gate[..., None].astype(h.dtype)
    return jax.vmap(lambda yb, ib: jnp.zeros((T, D), yb.dtype).at[ib.reshape(-1)].add(yb.reshape(-1, D)))(y, idx)


def setup_inputs(seed: int = 0) -> dict:
    key = jax.random.key(seed)
    ks = jax.random.split(key, 32)
    L = DEPTH

    def nrm(k, shape, s):
        return s * jax.random.normal(k, shape, jnp.float32)

    def affine(k, width):
        kg, kb = jax.random.split(k)
        return jnp.stack([1.0 + nrm(kg, (L, width), 0.1), nrm(kb, (L, width), 0.01)], axis=1)

    return {
        'x': nrm(ks[0], (BATCH, SEQ, D_MODEL), 1.0),
        'c': nrm(ks[1], (BATCH, D_MODEL), 1.0),
        'ctx': nrm(ks[2], (BATCH, CTX_LEN, D_MODEL), 1.0),
        'c_ctx': nrm(ks[3], (D_MODEL,), 1.0),
        'w_ada': nrm(ks[4], (L, D_MODEL, 6 * D_MODEL), 0.5 * D_MODEL ** -0.5),
        'b_ada': nrm(ks[5], (L, 6 * D_MODEL), 0.01),
        'w_in': nrm(ks[6], (L, D_MODEL, IN_COLS), D_MODEL ** -0.5),
        'w_out': nrm(ks[7], (L, MIX_W, D_MODEL), DN_BETA * MIX_W ** -0.5),
        'a_mu': jax.random.uniform(ks[8], (L, 2, A_COLS), jnp.float32, 0.0, 0.5),
        'a_w0': nrm(ks[9], (L, 2, A_W), 0.5),
        'a_w2': nrm(ks[10], (L, 2, A_DECAY_RANK, A_W), A_DECAY_RANK ** -0.5),
        'a_a0': nrm(ks[11], (L, 2, A_W), 0.5),
        'a_a2': nrm(ks[12], (L, 2, A_ICLR_RANK, A_W), A_ICLR_RANK ** -0.5),
        'a_g2': nrm(ks[13], (L, A_GATE_RANK, A_W), A_GATE_RANK ** -0.5),
        'a_kk': 1.0 + nrm(ks[14], (L, A_W), 0.1),
        'a_ka': 1.0 + nrm(ks[15], (L, A_W), 0.1),
        'a_rk': nrm(ks[16], (L, A_HEADS, A_HD), 0.1),
        'a_gn': affine(ks[17], A_W),
        'b_gn': affine(ks[18], B_W),
        'c_rpb': nrm(ks[19], (L, C_HEADS, 2 * WIN_R - 1, 2 * WIN_C - 1), 0.1),
        'd_a2': nrm(ks[20], (L, 2, D_RANK, D_HEADS * D_QK), D_RANK ** -0.5),
        'd_ab': nrm(ks[21], (L, 2, D_HEADS * D_QK), 0.01),
        'd_gn': affine(ks[22], D_W),
        'ln1': affine(ks[23], D_MODEL),
        'w_router': nrm(ks[24], (L, D_MODEL, N_EXPERTS), D_MODEL ** -0.5),
        'w_e1': nrm(ks[25], (L, N_EXPERTS, D_MODEL, EXPERT_FF), D_MODEL ** -0.5),
        'w_e3': nrm(ks[26], (L, N_EXPERTS, D_MODEL, EXPERT_FF), D_MODEL ** -0.5),
        'w_e2': nrm(ks[27], (L, N_EXPERTS, EXPERT_FF, D_MODEL), DN_BETA * EXPERT_FF ** -0.5),
        'ln2': affine(ks[28], D_MODEL),
    }


def reference(x, c, ctx, c_ctx, w_ada, b_ada, w_in, w_out, a_mu, a_w0, a_w2, a_a0, a_a2, a_g2,
              a_kk, a_ka, a_rk, a_gn, b_gn, c_rpb, d_a2, d_ab, d_gn, ln1, w_router, w_e1, w_e3, w_e2, ln2):
    xc = ctx
    silu_c, silu_cc = jax.nn.silu(c), jax.nn.silu(c_ctx)
    for l in range(DEPTH):
        emit_ctx = l < DEPTH - 1
        sh1, sc1, g1, sh2, sc2, g2 = (m[:, None, :] for m in jnp.split(silu_c @ w_ada[l] + b_ada[l], 6, axis=-1))
        n_mod = 6 if emit_ctx else 2
        mod_ctx = jnp.split(silu_cc @ w_ada[l][:, :n_mod * D_MODEL] + b_ada[l][:n_mod * D_MODEL], n_mod)
        y_ctx, y_lat = token_mixer(modulate(xc, mod_ctx[0], mod_ctx[1]), modulate(x, sh1, sc1),
                                   w_in[l], w_out[l], a_mu[l], a_w0[l], a_w2[l], a_a0[l], a_a2[l], a_g2[l],
                                   a_kk[l], a_ka[l], a_rk[l], a_gn[l], b_gn[l], c_rpb[l], d_a2[l], d_ab[l],
                                   d_gn[l], emit_ctx)
        x = post_norm(x, g1 * y_lat, ln1[l])
        x = post_norm(x, g2 * expert_choice_ffn(modulate(x, sh2, sc2), w_router[l], w_e1[l], w_e3[l], w_e2[l]), ln2[l])
        if emit_ctx:
            xc = post_norm(xc, mod_ctx[2] * y_ctx, ln1[l])
            y_ffn_ctx = expert_choice_ffn(modulate(xc, mod_ctx[3], mod_ctx[4]), w_router[l], w_e1[l], w_e3[l], w_e2[l])
            xc = post_norm(xc, mod_ctx[5] * y_ffn_ctx, ln2[l])
    return x
```

```python
import numpy as np
from contextlib import ExitStack
import concourse.bass as bass
import concourse.mybir as mybir
from concourse.bass_utils import run_bass_kernel_spmd

F32 = mybir.dt.float32
I32 = mybir.dt.int32
AF = mybir.ActivationFunctionType
ALU = mybir.AluOpType
AX = mybir.AxisListType

SEM_LIMIT = 30000
N_DMA_SLOTS = 12


class Dep:
    __slots__ = ("w", "r", "rd")

    def __init__(self):
        self.w = None
        self.r = {}
        self.rd = []


class SemCounter:
    def __init__(self, K):
        self.K = K
        self.idx = K.new_sem()
        self.val = 0

    def bump(self, inc):
        if self.val + inc > SEM_LIMIT:
            self.idx = self.K.new_sem()
            self.val = 0
        self.val += inc
        return (self.idx, self.val)


class Eng:
    def __init__(self, K, name, h):
        self.K, self.name, self.h = K, name, h
        self.ctr = SemCounter(K)
        self.seen = {}
        self.last = None
        self.slots = None
        self.slot_i = 0

    def wait(self, tok):
        idx, val = tok
        if self.seen.get(idx, 0) >= val:
            return
        self.h.wait_ge(self.K.sems[idx], val)
        self.seen[idx] = val


class T:
    def __init__(self, h):
        self.h = h
        self.dep = Dep()

    def __getitem__(self, k):
        return self.h[k]


class Kern:
    def __init__(self, nc):
        self.nc = nc
        self.stack = ExitStack()
        self.sems = []
        self.eng = {}
        for name, h in (("pe", nc.tensor), ("dve", nc.vector), ("act", nc.scalar),
                        ("pool", nc.gpsimd), ("sp", nc.sync)):
            self.eng[name] = Eng(self, name, h)
        self.uid = 0

    def new_sem(self):
        s = self.stack.enter_context(self.nc.semaphore(f"s{len(self.sems)}"))
        self.sems.append(s)
        return len(self.sems) - 1

    def sb(self, st, shape, dtype=F32, name=None):
        self.uid += 1
        return T(st.enter_context(self.nc.sbuf_tensor(f"{name or 'sb'}_{self.uid}", list(shape), dtype)))

    def ps(self, st, shape, dtype=F32, name=None):
        self.uid += 1
        return T(st.enter_context(self.nc.psum_tensor(f"{name or 'ps'}_{self.uid}", list(shape), dtype)))

    def dram(self, name, shape, dtype=F32, kind="Internal"):
        return self.nc.dram_tensor(name, list(shape), dtype, kind=kind).ap()

    def _deps(self, E, reads, writes, is_dma):
        for d in reads:
            d = d.dep if isinstance(d, T) else d
            if d.w is not None:
                self._need(E, d.w, is_dma)
        for d in writes:
            d = d.dep if isinstance(d, T) else d
            if d.w is not None:
                self._need(E, d.w, is_dma)
            for en, tok in d.r.items():
                self._need(E, (en, tok), is_dma)
            for tok in d.rd:
                E.wait(tok)

    def _need(self, E, w, is_dma):
        en, tok = w
        if en == E.name and not is_dma and en == "pe":
            return
        E.wait(tok)

    def _record(self, en, tok, reads, writes, is_dma):
        for d in reads:
            d = d.dep if isinstance(d, T) else d
            if is_dma:
                d.rd.append(tok)
                if len(d.rd) > 64:
                    d.rd = d.rd[-64:]
            else:
                d.r[en] = tok
        for d in writes:
            d = d.dep if isinstance(d, T) else d
            d.w = (None if is_dma else en, tok)
            d.r = {}
            d.rd = []

    def op(self, en, fn, reads=(), writes=()):
        E = self.eng[en]
        self._deps(E, reads, writes, False)
        ins = fn(E.h)
        tok = E.ctr.bump(1)
        ins.then_inc(self.sems[tok[0]], 1)
        if en == "pe":
            E.seen[tok[0]] = tok[1]
        E.last = tok
        self._record(en, tok, reads, writes, False)
        return tok

    def dma(self, q, out, in_, reads=(), writes=(), **kw):
        Q = self.eng[q]
        self._deps(Q, reads, writes, True)
        if Q.slots is None:
            Q.slots = [[SemCounter(self), None] for _ in range(N_DMA_SLOTS)]
        sl = Q.slots[Q.slot_i]
        Q.slot_i = (Q.slot_i + 1) % N_DMA_SLOTS
        if sl[1] is not None:
            Q.wait(sl[1])
        ins = Q.h.dma_start(out=out, in_=in_, **kw)
        tok = sl[0].bump(16)
        ins.then_inc(self.sems[tok[0]], 16)
        sl[1] = tok
        self._record(None, tok, reads, writes, True)
        return tok

    def barrier(self):
        toks = []
        for E in self.eng.values():
            if E.last is not None:
                toks.append(E.last)
            if E.slots:
                for sl in E.slots:
                    if sl[1] is not None:
                        toks.append(sl[1])
        for E in self.eng.values():
            for t in toks:
                E.wait(t)

    def mm(self, out, lhsT, rhs, start, stop, reads, writes):
        return self.op("pe", lambda e: e.matmul(out, lhsT, rhs, start=start, stop=stop), reads, writes)

    def tr(self, out, in_, ident, reads, writes):
        return self.op("pe", lambda e: e.transpose(out, in_, ident), reads, writes)


D = 4096
SEQ = 2048
CTX = 256
TT = SEQ + CTX
NT = TT // 128
KC = D // 128
DEPTH = 2
A_COLS, B_COLS, C_COLS, D_COLS = 3488, 3072, 3072, 3104
IN_COLS = A_COLS + B_COLS + C_COLS + D_COLS
OFF_A, OFF_B, OFF_C, OFF_D = 0, A_COLS, A_COLS + B_COLS, A_COLS + B_COLS + C_COLS
DN_ALPHA = (2 * DEPTH) ** 0.25
NE = 16
FF = 1024

WEIGHT_SPECS = [
    ("w_ada", [DEPTH, D, 6 * D]), ("b_ada", [DEPTH, 6 * D]), ("w_in", [DEPTH, D, IN_COLS]),
    ("w_out", [DEPTH, D, D]), ("a_mu", [DEPTH, 2, A_COLS]), ("a_w0", [DEPTH, 2, 1024]),
    ("a_w2", [DEPTH, 2, 64, 1024]), ("a_a0", [DEPTH, 2, 1024]), ("a_a2", [DEPTH, 2, 64, 1024]),
    ("a_g2", [DEPTH, 160, 1024]), ("a_kk", [DEPTH, 1024]), ("a_ka", [DEPTH, 1024]),
    ("a_rk", [DEPTH, 16, 64]), ("a_gn", [DEPTH, 2, 1024]), ("b_gn", [DEPTH, 2, 1024]),
    ("d_a2", [DEPTH, 2, 16, 512]), ("d_ab", [DEPTH, 2, 512]), ("d_gn", [DEPTH, 2, 1024]),
    ("ln1", [DEPTH, 2, D]), ("w_router", [DEPTH, D, NE]), ("w_e1", [DEPTH, NE, D, FF]),
    ("w_e3", [DEPTH, NE, D, FF]), ("w_e2", [DEPTH, NE, FF, D]), ("ln2", [DEPTH, 2, D]),
]


class Prog:
    def __init__(self, dbg=()):
        self.nc = nc = bass.Bass("TRN2", target_bir_lowering=False)
        self.K = Kern(nc)
        self.dbg = set(dbg)
        self.inp = {}
        self.scr = {}

    def ext_in(self, name, shape, dtype=F32):
        self.inp[name] = self.nc.dram_tensor(name, list(shape), dtype, kind="ExternalInput").ap()
        return self.inp[name]

    def scratch(self, name, shape, dtype=F32):
        kind = "ExternalOutput" if name in self.dbg else "Internal"
        self.scr[name] = self.nc.dram_tensor(name, list(shape), dtype, kind=kind).ap()
        return self.scr[name]

    def consts(self, st):
        K = self.K
        self.ident = K.sb(st, [128, 128], name="ident")
        K.op("pool", lambda e: e.memset(self.ident[:], 0.0), writes=[self.ident])
        K.op("pool", lambda e: e.affine_select(out=self.ident[:], in_=self.ident[:], pattern=[[-1, 128]],
                                               compare_op=ALU.not_equal, fill=1.0, base=0, channel_multiplier=1),
             reads=[self.ident], writes=[self.ident])
        self.eps6 = K.sb(st, [128, 1], name="eps6")
        K.op("pool", lambda e: e.memset(self.eps6[:], 1e-6), writes=[self.eps6])

    def phase_ada(self, l):
        K = self.K
        w = self.inp["w_ada"]
        modsf = self.scr[f"mods{l}"]
        mods = modsf.rearrange("(r m) f -> r (m f)", r=2)
        NB = 256
        with ExitStack() as st:
            sc = K.sb(st, [128, KC, 2], name="sc")
            K.dma("sp", sc[:], self.inp["cc"].rearrange("(kc p) r -> p kc r", p=128), writes=[sc])
            K.op("act", lambda e: e.activation(sc[:], sc[:], AF.Silu), reads=[sc], writes=[sc])
            wt = [K.sb(st, [128, KC, NB], name="wada") for _ in range(2)]
            pp = [K.ps(st, [2, NB], name="pada") for _ in range(2)]
            ot = [K.sb(st, [2, NB], name="oada") for _ in range(2)]
            nblk = 6 * D // NB
            for nb in range(nblk):
                wb, p, o = wt[nb % 2], pp[nb % 2], ot[nb % 2]
                K.dma("sp" if nb % 2 == 0 else "pool", wb[:],
                      w[l, :, nb * NB:(nb + 1) * NB].rearrange("(kc p) n -> p kc n", p=128), writes=[wb])
                for kc in range(KC):
                    K.mm(p[:], sc[:, kc, :], wb[:, kc, :], kc == 0, kc == KC - 1, reads=[sc, wb], writes=[p])
                K.op("dve", lambda e: e.tensor_copy(o[:], p[:]), reads=[p], writes=[o])
                K.dma("sp", mods[:, nb * NB:(nb + 1) * NB], o[:], reads=[o])
            K.barrier()
            mt = K.sb(st, [128, 384], name="mt")
            bt = K.sb(st, [128, 384], name="bt")
            K.dma("sp", mt[:], modsf.rearrange("(q a) f -> q (a f)", a=3), writes=[mt])
            bview = self.inp["b_ada"][l].rearrange("(j f) -> j f", f=384)
            K.dma("sp", bt[0:64, :], bview, writes=[bt])
            K.dma("sp", bt[64:128, :], bview, writes=[bt])
            K.op("dve", lambda e: e.tensor_tensor(mt[:], mt[:], bt[:], ALU.add), reads=[mt, bt], writes=[mt])
            K.dma("sp", modsf.rearrange("(q a) f -> q (a f)", a=3), mt[:], reads=[mt])
            K.barrier()

    def load_modp(self, st, l):
        K = self.K
        mods = self.scr[f"mods{l}"]
        modp = K.sb(st, [128, 2 * 6 * KC], name="modp")
        with ExitStack() as s2:
            rows = K.sb(s2, [128, 3, 128], name="mrows")
            K.dma("sp", rows[:], mods.rearrange("(a q) f -> q a f", q=128), writes=[rows])
            pt = K.ps(s2, [128, 3, 128], name="mpt")
            for a in range(3):
                K.tr(pt[:, a, :], rows[:, a, :], self.ident[:], reads=[rows, self.ident], writes=[pt])
            K.op("dve", lambda e: e.tensor_copy(modp[:], pt[:].rearrange("p a q -> p (a q)")), reads=[pt], writes=[modp])
            K.barrier()
        return modp

    @staticmethod
    def mp(modp, r, c, kc):
        j = r * 192 + c * 32 + kc
        return modp[:, j:j + 1]

    def ln_tile(self, st_bufs, xt, eps_t):
        K = self.K
        stats, mv, rstd = st_bufs
        for j in range(8):
            K.op("dve", lambda e: e.bn_stats(stats[:, j, :], xt[:, j * 512:(j + 1) * 512]), reads=[xt], writes=[stats])
        K.op("dve", lambda e: e.bn_aggr(mv[:], stats[:]), reads=[stats], writes=[mv])
        K.op("act", lambda e: e.activation(rstd[:], mv[:, 1:2], AF.Sqrt, bias=eps_t[:, 0:1], scale=1.0),
             reads=[mv, eps_t], writes=[rstd])
        K.op("dve", lambda e: e.reciprocal(rstd[:], rstd[:]), reads=[rstd], writes=[rstd])
        K.op("dve", lambda e: e.tensor_scalar(xt[:], xt[:], mv[:, 0:1], rstd[:, 0:1], ALU.subtract, ALU.mult),
             reads=[xt, mv, rstd], writes=[xt])

    def phase_modT(self, l, src, dst, c_sh, c_sc, tiles):
        K = self.K
        with ExitStack() as st:
            modp = self.load_modp(st, l)
            sc1 = K.sb(st, [128, 2 * 6 * KC], name="sc1")
            K.op("dve", lambda e: e.tensor_scalar(sc1[:], modp[:], 1.0, None, ALU.add), reads=[modp], writes=[sc1])
            xts = [K.sb(st, [128, D], name="xt") for _ in range(2)]
            hts = [K.sb(st, [128, KC, 128], name="ht") for _ in range(2)]
            bufs = [(K.sb(st, [128, 8, 6]), K.sb(st, [128, 2]), K.sb(st, [128, 1])) for _ in range(2)]
            pts = [K.ps(st, [128, 4, 128], name="ptr") for _ in range(4)]
            for i, ti in enumerate(tiles):
                r = 1 if ti < CTX // 128 else 0
                xt, ht, bf = xts[i % 2], hts[i % 2], bufs[i % 2]
                K.dma("sp", xt[:], src[ti * 128:(ti + 1) * 128, :], writes=[xt])
                self.ln_tile(bf, xt, self.eps6)
                for g in range(KC // 4):
                    pt = pts[g % 4]
                    for j in range(4):
                        kc = g * 4 + j
                        K.tr(pt[:, j, :], xt[:, kc * 128:(kc + 1) * 128], self.ident[:], reads=[xt, self.ident], writes=[pt])
                    for j in range(4):
                        kc = g * 4 + j
                        if j % 2 == 0:
                            K.op("act", lambda e: e.activation(ht[:, kc, :], pt[:, j, :], AF.Identity,
                                                               bias=self.mp(modp, r, c_sh, kc), scale=self.mp(sc1, r, c_sc, kc)),
                                 reads=[pt, modp, sc1], writes=[ht])
                        else:
                            K.op("dve", lambda e: e.tensor_scalar(ht[:, kc, :], pt[:, j, :], self.mp(sc1, r, c_sc, kc),
                                                                  self.mp(modp, r, c_sh, kc), ALU.mult, ALU.add),
                                 reads=[pt, modp, sc1], writes=[ht])
                K.dma("pool", dst[:, ti * 128:(ti + 1) * 128].rearrange("(kc p) t -> p kc t", p=128), ht[:], reads=[ht])
            K.barrier()

    def gemm_fm(self, W, inT, outT, Kd, N, tok0, tok1, evac=None):
        K = self.K
        kcs = Kd // 128
        NB = 256
        TB = 512
        with ExitStack() as st:
            hb = K.sb(st, [128, kcs, TB], name="g_in")
            wbs = [K.sb(st, [128, kcs, NB], name="g_w") for _ in range(2)]
            pps = [K.ps(st, [128, TB], name="g_ps") for _ in range(2)]
            ots = [K.sb(st, [128, TB], name="g_o") for _ in range(3)]
            cnt = 0
            wcnt = 0
            for t0 in range(tok0, tok1, TB):
                tw = min(TB, tok1 - t0)
                K.dma("sp", hb[:, :, :tw], inT[:, t0:t0 + tw].rearrange("(kc p) t -> p kc t", p=128), writes=[hb])
                for n0 in range(0, N, NB):
                    nw = min(NB, N - n0)
                    wb = wbs[wcnt % 2]
                    K.dma("sp" if wcnt % 2 == 0 else "pool", wb[:, :, :nw],
                          W[:, n0:n0 + nw].rearrange("(kc p) n -> p kc n", p=128), writes=[wb])
                    wcnt += 1
                    for j0 in range(0, nw, 128):
                        jw = min(128, nw - j0)
                        p, o = pps[cnt % 2], ots[cnt % 3]
                        for kc in range(kcs):
                            K.mm(p[:jw, :tw], wb[:, kc, j0:j0 + jw], hb[:, kc, :tw], kc == 0, kc == kcs - 1,
                                 reads=[wb, hb], writes=[p])
                        if cnt % 2 == 0:
                            K.op("act", lambda e: e.copy(o[:jw, :tw], p[:jw, :tw]), reads=[p], writes=[o])
                        else:
                            K.op("dve", lambda e: e.tensor_copy(o[:jw, :tw], p[:jw, :tw]), reads=[p], writes=[o])
                        K.dma("pool" if cnt % 2 == 0 else "sp", outT[n0 + j0:n0 + j0 + jw, t0:t0 + tw], o[:jw, :tw], reads=[o])
                        cnt += 1
            K.barrier()


def build_program(phases=None, dbg=(), layers=(0, 1)):
    P = Prog(dbg)
    K = P.K
    P.ext_in("x", [SEQ, D])
    P.ext_in("ctx", [CTX, D])
    P.ext_in("cc", [D, 2])
    for name, shape in WEIGHT_SPECS:
        P.ext_in(name, shape)
    for l in range(DEPTH):
        P.scratch(f"mods{l}", [384, 128])
    P.scratch("hT", [D, TT])
    P.scratch("zT", [IN_COLS, TT])
    P.scratch("xcur", [TT, D])
    with K.stack, ExitStack() as st:
        P.consts(st)
        K.dma("sp", P.scr["xcur"][0:CTX, :], P.inp["ctx"])
        K.dma("pool", P.scr["xcur"][CTX:TT, :], P.inp["x"])
        K.barrier()
        for l in layers:
            if phases is None or "ada" in phases:
                P.phase_ada(l)
            if phases is None or "modT" in phases:
                P.phase_modT(l, P.scr["xcur"], P.scr["hT"], 0, 1, list(range(NT)))
            if phases is None or "win" in phases:
                P.gemm_fm(P.inp["w_in"][l], P.scr["hT"], P.scr["zT"], D, IN_COLS, 0, TT)
        K.barrier()
    return P


GRID_W = 64
MASKV = -30000.0


def na_tables(c_rpb):
    L, H = c_rpb.shape[0], c_rpb.shape[1]
    kc = np.arange(64)[:, None]
    qc = np.arange(64)[None, :]
    start = np.clip(qc - 8, 0, 48)
    valid = (kc >= start) & (kc < start + 16)
    dc = np.clip(kc - qc + 15, 0, 30)
    tb = np.full((L, H, 64, 16, 64), MASKV, np.float32)
    for dr in range(15):
        g = c_rpb[:, :, dr, :][:, :, dc]
        tb[:, :, :, dr, :] = np.where(valid[None, None], g, np.float32(MASKV))
    return tb


def na_row_plan(i):
    rows = SEQ // GRID_W
    q_rows = (2 * i, 2 * i + 1)
    starts = [int(np.clip(qr - 4, 0, rows - 8)) for qr in q_rows]
    r0 = starts[0]
    r1 = starts[1] + 8
    plan = []
    for kr in range(r0, r1):
        plan.append([(kr - qr + 7) if (st <= kr < st + 8) else 15 for qr, st in zip(q_rows, starts)])
    return r0, r1 - r0, plan


def phase_natten(P, l, emit_ctx):
    K = P.K
    zT = P.scr["zT"]
    ycat = P.scr["ycat"]
    natb = P.inp["natb"]
    with ExitStack() as st:
        qT = K.sb(st, [64, TT], name="na_q")
        kT = K.sb(st, [64, TT], name="na_k")
        vT = K.sb(st, [64, TT], name="na_v")
        V = K.sb(st, [128, NT, 65], name="na_V")
        tab = K.sb(st, [128, 16, 64], name="na_tab")
        K.op("pool", lambda e: e.memset(V[:], 1.0), writes=[V])
        pss = [K.ps(st, [128, 128], name="na_ps") for _ in range(2)]
        pso = [K.ps(st, [128, 65], name="na_po") for _ in range(2)]
        ptr = K.ps(st, [128, 64], name="na_ptr")
        tmp = [K.sb(st, [128, 128], name="na_tmp") for _ in range(2)]
        pT = [K.sb(st, [128, 128], name="na_pT") for _ in range(2)]
        rc = [K.sb(st, [128, 1], name="na_rc") for _ in range(2)]
        yo = [K.sb(st, [128, 64], name="na_yo") for _ in range(2)]
        cnt = 0
        for h in range(16):
            K.dma("sp", qT[:], zT[OFF_C + h * 64:OFF_C + (h + 1) * 64, :], writes=[qT])
            K.dma("sp", kT[:], zT[OFF_C + 1024 + h * 64:OFF_C + 1024 + (h + 1) * 64, :], writes=[kT])
            K.dma("pool", vT[:], zT[OFF_C + 2048 + h * 64:OFF_C + 2048 + (h + 1) * 64, :], writes=[vT])
            K.dma("pool", tab[0:64], natb[l, h], writes=[tab])
            K.dma("pool", tab[64:128], natb[l, h], writes=[tab])
            for ti in range(NT):
                K.tr(ptr[:], vT[:, ti * 128:(ti + 1) * 128], P.ident[0:64, 0:64], reads=[vT, P.ident], writes=[ptr])
                K.op("act", lambda e: e.copy(V[:, ti, 0:64], ptr[:]), reads=[ptr], writes=[V])
            qtiles = list(range(2, NT)) + ([0, 1] if emit_ctx else [])
            for qi in qtiles:
                q_ap = qT[:, qi * 128:(qi + 1) * 128]
                chunks = []
                if qi >= 2:
                    r0, nr, plan = na_row_plan(qi - 2)
                    for c in range((nr + 1) // 2):
                        kw = 128 if 2 * c + 1 < nr else 64
                        chunks.append((2 + r0 // 2 + c, kw, plan[2 * c:2 * c + 2]))
                chunks += [(0, 128, None), (1, 128, None)]
                po = pso[cnt % 2]
                for ci, (kt, kw, pl) in enumerate(chunks):
                    ps, tm, pt = pss[cnt % 2], tmp[cnt % 2], pT[cnt % 2]
                    cnt += 1
                    K.mm(ps[:kw, :], kT[:, kt * 128:kt * 128 + kw], q_ap, True, True, reads=[kT, qT], writes=[ps])
                    if pl is None:
                        K.op("act", lambda e: e.activation(pt[:kw, :], ps[:kw, :], AF.Exp, scale=0.125), reads=[ps], writes=[pt])
                    else:
                        for a in range(len(pl)):
                            for b in range(2):
                                K.op("dve", lambda e: e.scalar_tensor_tensor(
                                    out=tm[a * 64:(a + 1) * 64, b * 64:(b + 1) * 64], in0=ps[a * 64:(a + 1) * 64, b * 64:(b + 1) * 64],
                                    scalar=0.125, in1=tab[a * 64:(a + 1) * 64, pl[a][b], :], op0=ALU.mult, op1=ALU.add),
                                    reads=[ps, tab], writes=[tm])
                        K.op("act", lambda e: e.activation(pt[:kw, :], tm[:kw, :], AF.Exp), reads=[tm], writes=[pt])
                    K.mm(po[:], pt[:kw, :], V[:kw, kt, :], ci == 0, ci == len(chunks) - 1, reads=[pt, V], writes=[po])
                r_, y_ = rc[cnt % 2], yo[cnt % 2]
                K.op("dve", lambda e: e.reciprocal(r_[:], po[:, 64:65]), reads=[po], writes=[r_])
                K.op("dve", lambda e: e.tensor_scalar(y_[:], po[:, 0:64], r_[:, 0:1], None, ALU.mult), reads=[po, r_], writes=[y_])
                K.dma("sp", ycat[qi * 128:(qi + 1) * 128, 2048 + h * 64:2048 + (h + 1) * 64], y_[:], reads=[y_])
        K.barrier()


def rope_tables():
    p = np.arange(128)
    t = np.arange(SEQ)
    j = p % 32
    inv = (10000.0 ** (-(2.0 * j) / 64.0)).astype(np.float32)
    pos = np.where(p[:, None] < 64, t[None, :] // GRID_W, t[None, :] % GRID_W).astype(np.float32)
    ang = (pos * inv[:, None]).astype(np.float32)
    C = np.cos(ang).astype(np.float32)
    S = np.sin(ang).astype(np.float32)
    u1 = (p % 64) < 32
    S = np.where(u1[:, None], -S, S).astype(np.float32)
    partner = np.where(u1, p + 32, p - 32)
    permT = np.zeros((128, 128), np.float32)
    permT[partner, p] = 1.0
    return C, S, permT


def ret_tables():
    h = np.arange(4)
    lgf = np.log1p(-np.exp2(-5.0 - h)).astype(np.float32)
    lgb = lgf[::-1]
    sp = np.arange(128)[:, None]
    tp = np.arange(128)[None, :]
    tb = np.zeros((4, 128, 63, 128), np.float32)
    scale = np.float32(128.0 ** -0.5)
    for hh in range(4):
        for idx in range(31):
            dl = 128 * (idx - 15) + tp - sp
            f = np.where(dl > 0, np.exp(lgf[hh] * np.maximum(dl, 0)), np.where(dl < 0, np.exp(lgb[hh] * np.maximum(-dl, 0)), 2.0))
            tb[hh, :, idx, :] = f
        for js in range(2):
            for it in range(16):
                jj = js * 128 + sp
                tt = it * 128 + tp
                tb[hh, :, 31 + js * 16 + it, :] = np.exp(lgf[hh] * (tt + CTX - jj)) + np.exp(lgb[hh] * (SEQ - tt + jj))
    return (tb * scale).astype(np.float32)


def gn_readout(P, st_bufs, src_ps, width, nh, eps_t, gain_bc, bias_bc, out_sb):
    K = P.K
    stats, mv, rstd = st_bufs
    for hh in range(nh):
        sl = slice(hh * width, (hh + 1) * width)
        K.op("dve", lambda e: e.bn_stats(stats[:, hh, :], src_ps[:, sl]), reads=[src_ps], writes=[stats])
        K.op("dve", lambda e: e.bn_aggr(mv[:, hh, :], stats[:, hh, :]), reads=[stats], writes=[mv])
    K.op("act", lambda e: e.activation(rstd[:, 0:nh], mv[:, 0:nh, 1], AF.Sqrt, bias=eps_t[:, 0:1], scale=1.0),
         reads=[mv, eps_t], writes=[rstd])
    K.op("dve", lambda e: e.reciprocal(rstd[:, 0:nh], rstd[:, 0:nh]), reads=[rstd], writes=[rstd])
    for hh in range(nh):
        sl = slice(hh * width, (hh + 1) * width)
        K.op("dve", lambda e: e.tensor_scalar(out_sb[:, sl], src_ps[:, sl], mv[:, hh, 0:1], rstd[:, hh:hh + 1], ALU.subtract, ALU.mult),
             reads=[src_ps, mv, rstd], writes=[out_sb])
    K.op("pool", lambda e: e.tensor_tensor(out_sb[:], out_sb[:], gain_bc[:], ALU.mult), reads=[out_sb, gain_bc], writes=[out_sb])
    K.op("pool", lambda e: e.tensor_tensor(out_sb[:], out_sb[:], bias_bc[:], ALU.add), reads=[out_sb, bias_bc], writes=[out_sb])


def phase_retention(P, l, emit_ctx):
    K = P.K
    zT, ycat = P.scr["zT"], P.scr["ycat"]
    with ExitStack() as st:
        ropeC = K.sb(st, [128, SEQ], name="ropeC")
        ropeS = K.sb(st, [128, SEQ], name="ropeS")
        permT = K.sb(st, [128, 128], name="permT")
        eps5 = K.sb(st, [128, 1], name="eps5")
        K.op("pool", lambda e: e.memset(eps5[:], 1e-5), writes=[eps5])
        K.dma("sp", ropeC[:], P.inp["ropeC"], writes=[ropeC])
        K.dma("sp", ropeS[:], P.inp["ropeS"], writes=[ropeS])
        K.dma("sp", permT[:], P.inp["permT"], writes=[permT])
        qk = [K.sb(st, [128, TT], name="rt_q"), K.sb(st, [128, TT], name="rt_k")]
        vT = K.sb(st, [128, 2, TT], name="rt_vT")
        gT = K.sb(st, [128, 2, TT], name="rt_gT")
        V = K.sb(st, [128, NT, 256], name="rt_V")
        tab = K.sb(st, [128, 63, 128], name="rt_tab")
        gain = K.sb(st, [128, 256], name="rt_gain")
        bias = K.sb(st, [128, 256], name="rt_bias")
        rtmp = K.sb(st, [128, 512], name="rt_rtmp")
        psr = K.ps(st, [128, 512], name="rt_psr")
        pss = [K.ps(st, [128, 128], name="rt_ps") for _ in range(2)]
        psy = [K.ps(st, [128, 256], name="rt_py") for _ in range(2)]
        psg = K.ps(st, [128, 256], name="rt_pg")
        ms = [K.sb(st, [128, 128], name="rt_m") for _ in range(3)]
        yn = [K.sb(st, [128, 256], name="rt_yn") for _ in range(2)]
        gs = [K.sb(st, [128, 256], name="rt_gs") for _ in range(2)]
        bufs = [(K.sb(st, [128, 1, 6]), K.sb(st, [128, 1, 2]), K.sb(st, [128, 1])) for _ in range(2)]
        cnt = 0
        for h in range(4):
            K.dma("sp", qk[0][:], zT[OFF_B + h * 128:OFF_B + (h + 1) * 128, :], writes=[qk[0]])
            K.dma("sp", qk[1][:], zT[OFF_B + 512 + h * 128:OFF_B + 512 + (h + 1) * 128, :], writes=[qk[1]])
            K.dma("pool", vT[:], zT[OFF_B + 1024 + h * 256:OFF_B + 1024 + (h + 1) * 256, :].rearrange("(a p) t -> p a t", p=128), writes=[vT])
            K.dma("pool", gT[:], zT[OFF_B + 2048 + h * 256:OFF_B + 2048 + (h + 1) * 256, :].rearrange("(a p) t -> p a t", p=128), writes=[gT])
            K.dma("sp", tab[:], P.inp["ret_tab"][h], writes=[tab])
            K.dma("sp", gain[:], P.inp["b_gn"][l, 0, h * 256:(h + 1) * 256].partition_broadcast(128), writes=[gain])
            K.dma("sp", bias[:], P.inp["b_gn"][l, 1, h * 256:(h + 1) * 256].partition_broadcast(128), writes=[bias])
            for x in qk:
                for b0 in range(0, SEQ, 512):
                    xs = x[:, CTX + b0:CTX + b0 + 512]
                    K.mm(psr[:], permT[:], xs, True, True, reads=[permT, x], writes=[psr])
                    K.op("dve", lambda e: e.tensor_tensor(rtmp[:], psr[:], ropeS[:, b0:b0 + 512], ALU.mult), reads=[psr, ropeS], writes=[rtmp])
                    K.op("pool", lambda e: e.tensor_tensor(xs, xs, ropeC[:, b0:b0 + 512], ALU.mult), reads=[x, ropeC], writes=[x])
                    K.op("dve", lambda e: e.tensor_tensor(xs, xs, rtmp[:], ALU.add), reads=[x, rtmp], writes=[x])
            for ti in range(NT):
                for a in range(2):
                    K.tr(psg[:, a * 128:(a + 1) * 128], vT[:, a, ti * 128:(ti + 1) * 128], P.ident[:], reads=[vT, P.ident], writes=[psg])
                K.op("act", lambda e: e.copy(V[:, ti, :], psg[:]), reads=[psg], writes=[V])
            for qi in (list(range(NT)) if emit_ctx else list(range(2, NT))):
                stl = [0, 1] if qi < 2 else list(range(NT))
                py = psy[cnt % 2]
                for n, sj in enumerate(stl):
                    idx = (31 + sj * 16 + (qi - 2)) if (qi >= 2 and sj < 2) else (15 + qi - sj)
                    ps, m = pss[n % 2], ms[n % 3]
                    K.mm(ps[:], qk[1][:, sj * 128:(sj + 1) * 128], qk[0][:, qi * 128:(qi + 1) * 128], True, True, reads=qk, writes=[ps])
                    K.op("dve", lambda e: e.tensor_tensor(m[:], ps[:], tab[:, idx, :], ALU.mult), reads=[ps, tab], writes=[m])
                    K.mm(py[:], m[:], V[:, sj, :], n == 0, n == len(stl) - 1, reads=[m, V], writes=[py])
                y_, g_, bf = yn[cnt % 2], gs[cnt % 2], bufs[cnt % 2]
                cnt += 1
                gn_readout(P, bf, py, 256, 1, eps5, gain, bias, y_)
                for a in range(2):
                    K.tr(psg[:, a * 128:(a + 1) * 128], gT[:, a, qi * 128:(qi + 1) * 128], P.ident[:], reads=[gT, P.ident], writes=[psg])
                K.op("act", lambda e: e.activation(g_[:], psg[:], AF.Silu), reads=[psg], writes=[g_])
                K.op("dve", lambda e: e.tensor_tensor(y_[:], y_[:], g_[:], ALU.mult), reads=[y_, g_], writes=[y_])
                K.dma("sp", ycat[qi * 128:(qi + 1) * 128, 1024 + h * 256:1024 + (h + 1) * 256], y_[:], reads=[y_])
        K.barrier()


def make_tri(P, st, val, fwd, name):
    K = P.K
    t = K.sb(st, [128, 128], name=name)
    K.op("pool", lambda e: e.memset(t[:], val), writes=[t])
    if fwd:
        K.op("pool", lambda e: e.affine_select(out=t[:], in_=t[:], pattern=[[1, 128]], compare_op=ALU.is_ge, fill=0.0,
                                               base=0, channel_multiplier=-1), reads=[t], writes=[t])
    else:
        K.op("pool", lambda e: e.affine_select(out=t[:], in_=t[:], pattern=[[-1, 128]], compare_op=ALU.is_ge, fill=0.0,
                                               base=0, channel_multiplier=1), reads=[t], writes=[t])
    return t


def phase_gla(P, l, emit_ctx):
    K = P.K
    zT, ycat = P.scr["zT"], P.scr["ycat"]
    SC = 128.0 ** -0.5
    with ExitStack() as st:
        tri = [make_tri(P, st, -1.0 / 16.0, True, "tri_f"), make_tri(P, st, -1.0 / 16.0, False, "tri_b")]
        msk = [make_tri(P, st, 1.0, True, "msk_f"), make_tri(P, st, 1.0, False, "msk_b")]
        ones1 = K.sb(st, [1, 128], name="ones1")
        onec = K.sb(st, [128, 1], name="onec")
        eps5 = K.sb(st, [128, 1], name="eps5")
        K.op("pool", lambda e: e.memset(ones1[:], 1.0), writes=[ones1])
        K.op("pool", lambda e: e.memset(onec[:], 1.0), writes=[onec])
        K.op("pool", lambda e: e.memset(eps5[:], 1e-5), writes=[eps5])
        acT = [K.sb(st, [16, TT], name="gl_ac") for _ in range(2)]
        a2 = [K.sb(st, [16, 512], name="gl_a2") for _ in range(2)]
        ab = [K.sb(st, [1, 512], name="gl_ab") for _ in range(2)]
        for d in range(2):
            K.dma("sp", acT[d][:], zT[OFF_D + 3072 + d * 16:OFF_D + 3072 + (d + 1) * 16, :], writes=[acT[d]])
            K.dma("sp", a2[d][:], P.inp["d_a2"][l, d], writes=[a2[d]])
            K.dma("sp", ab[d][:], P.inp["d_ab"][l, d:d + 1, :], writes=[ab[d]])
        qT = K.sb(st, [128, TT], name="gl_q")
        kT = K.sb(st, [128, TT], name="gl_k")
        qh = [K.sb(st, [128, TT], name="gl_qh") for _ in range(2)]
        kh = [K.sb(st, [128, TT], name="gl_kh") for _ in range(2)]
        vT = K.sb(st, [128, 2, TT], name="gl_vT")
        gT = K.sb(st, [128, 2, TT], name="gl_gT")
        V = K.sb(st, [128, NT, 256], name="gl_V")
        gain = K.sb(st, [128, 256], name="gl_gain")
        bias = K.sb(st, [128, 256], name="gl_bias")
        lp = K.sb(st, [128, 128], name="gl_lp")
        ex = K.sb(st, [128, 128], name="gl_ex")
        eq = K.sb(st, [128, 128], name="gl_eq")
        ek = K.sb(st, [128, 128], name="gl_ek")
        gtot = [K.sb(st, [128, NT], name="gl_tot") for _ in range(2)]
        gstar = [K.sb(st, [128, NT], name="gl_star") for _ in range(2)]
        erow = [K.sb(st, [128, NT], name="gl_erow") for _ in range(2)]
        psl = K.ps(st, [128, 128], name="gl_psl")
        psc = K.ps(st, [128, 128], name="gl_psc")
        pss = [K.ps(st, [128, 128], name="gl_ps") for _ in range(2)]
        psy = [K.ps(st, [128, 256], name="gl_py") for _ in range(2)]
        psg = K.ps(st, [128, 256], name="gl_pg")
        qs = [K.sb(st, [128, 128], name="gl_qs") for _ in range(3)]
        ms = [K.sb(st, [128, 128], name="gl_m") for _ in range(3)]
        yn = [K.sb(st, [128, 256], name="gl_yn") for _ in range(2)]
        gs = [K.sb(st, [128, 256], name="gl_gs") for _ in range(2)]
        bufs = [(K.sb(st, [128, 1, 6]), K.sb(st, [128, 1, 2]), K.sb(st, [128, 1])) for _ in range(2)]
        order = [list(range(NT)), [1, 0] + list(range(NT - 1, 1, -1))]
        cnt = 0
        for h in range(4):
            K.dma("sp", qT[:], zT[OFF_D + h * 128:OFF_D + (h + 1) * 128, :], writes=[qT])
            K.dma("sp", kT[:], zT[OFF_D + 512 + h * 128:OFF_D + 512 + (h + 1) * 128, :], writes=[kT])
            K.dma("pool", vT[:], zT[OFF_D + 1024 + h * 256:OFF_D + 1024 + (h + 1) * 256, :].rearrange("(a p) t -> p a t", p=128), writes=[vT])
            K.dma("pool", gT[:], zT[OFF_D + 2048 + h * 256:OFF_D + 2048 + (h + 1) * 256, :].rearrange("(a p) t -> p a t", p=128), writes=[gT])
            K.dma("sp", gain[:], P.inp["d_gn"][l, 0, h * 256:(h + 1) * 256].partition_broadcast(128), writes=[gain])
            K.dma("sp", bias[:], P.inp["d_gn"][l, 1, h * 256:(h + 1) * 256].partition_broadcast(128), writes=[bias])
            import os
            GS = os.environ.get("GLA_STOP", "")
            if GS == "a":
                break
            for ti in range(NT):
                for a in range(2):
                    K.tr(psg[:, a * 128:(a + 1) * 128], vT[:, a, ti * 128:(ti + 1) * 128], P.ident[:], reads=[vT, P.ident], writes=[psg])
                K.op("act", lambda e: e.copy(V[:, ti, :], psg[:]), reads=[psg], writes=[V])
            if GS == "b":
                break
            for d in range(2):
                for ti in range(NT):
                    if GS == "c" and ti > 0:
                        break
                    tsl = slice(ti * 128, (ti + 1) * 128)
                    K.mm(psl[:], acT[d][:, tsl], a2[d][:, h * 128:(h + 1) * 128], True, False, reads=[acT[d], a2[d]], writes=[psl])
                    K.mm(psl[:], ones1[:], ab[d][:, h * 128:(h + 1) * 128], False, True, reads=[ones1, ab[d]], writes=[psl])
                    if GS == "d":
                        continue
                    K.op("act", lambda e: e.activation(ex[:], psl[:], AF.Exp, scale=-1.0), reads=[psl], writes=[ex])
                    K.op("act", lambda e: e.activation(lp[:], ex[:], AF.Ln, bias=onec[:, 0:1], scale=1.0), reads=[ex, onec], writes=[lp])
                    if GS == "e":
                        continue
                    K.mm(psc[:], lp[:], tri[d][:], True, True, reads=[lp, tri[d]], writes=[psc])
                    K.op("act", lambda e: e.activation(eq[:], psc[:], AF.Exp), reads=[psc], writes=[eq])
                    K.op("act", lambda e: e.activation(ek[:], psc[:], AF.Exp, scale=-1.0), reads=[psc], writes=[ek])
                    if GS == "f":
                        continue
                    last = 127 if d == 0 else 0
                    if GS != "g2":
                        K.op("act", lambda e: e.copy(gtot[d][:, ti:ti + 1], psc[:, last:last + 1]), reads=[psc], writes=[gtot[d]])
                    if GS != "g1":
                        K.op("dve", lambda e: e.scalar_tensor_tensor(out=qh[d][:, tsl], in0=qT[:, tsl], scalar=SC, in1=eq[:], op0=ALU.mult, op1=ALU.mult),
                             reads=[qT, eq], writes=[qh[d]])
                    if GS in ("g", "g1", "g2"):
                        continue
                    K.op("pool", lambda e: e.tensor_tensor(kh[d][:, tsl], kT[:, tsl], ek[:], ALU.mult), reads=[kT, ek], writes=[kh[d]])
                od = order[d]
                import os
                if GS in ("1", "c", "d", "e", "f", "g", "g1", "g2"):
                    continue
                K.op("dve", lambda e: e.memset(gstar[d][:, od[0]:od[0] + 1], 0.0), writes=[gstar[d]])
                for n in range(1, NT):
                    a_, b_ = od[n - 1], od[n]
                    K.op("dve", lambda e: e.tensor_tensor(gstar[d][:, b_:b_ + 1], gstar[d][:, a_:a_ + 1], gtot[d][:, a_:a_ + 1], ALU.add),
                         reads=[gstar[d], gtot[d]], writes=[gstar[d]])
            for qi in (list(range(NT)) if emit_ctx else list(range(2, NT))):
                if GS:
                    break
                py = psy[cnt % 2]
                plist = []
                for d in range(2):
                    pos = order[d].index(qi)
                    plist += [(d, j, False) for j in order[d][:pos]] + [(d, qi, True)]
                for d in range(2):
                    K.op("act", lambda e: e.activation(erow[d][:], gstar[d][:], AF.Exp, bias=gstar[d][:, qi:qi + 1], scale=-1.0),
                         reads=[gstar[d]], writes=[erow[d]])
                qsl = slice(qi * 128, (qi + 1) * 128)
                for n, (d, j, diag) in enumerate(plist):
                    ps, m, q_ = pss[n % 2], ms[n % 3], qs[n % 3]
                    if diag:
                        K.mm(ps[:], kh[d][:, qsl], qh[d][:, qsl], True, True, reads=[kh[d], qh[d]], writes=[ps])
                        K.op("dve", lambda e: e.tensor_tensor(m[:], ps[:], msk[d][:], ALU.mult), reads=[ps, msk[d]], writes=[m])
                    else:
                        K.op("pool", lambda e: e.tensor_scalar(q_[:], qh[d][:, qsl], erow[d][:, j:j + 1], None, ALU.mult),
                             reads=[qh[d], erow[d]], writes=[q_])
                        K.mm(ps[:], kh[d][:, j * 128:(j + 1) * 128], q_[:], True, True, reads=[kh[d], q_], writes=[ps])
                        K.op("act", lambda e: e.copy(m[:], ps[:]), reads=[ps], writes=[m])
                    K.mm(py[:], m[:], V[:, j, :], n == 0, n == len(plist) - 1, reads=[m, V], writes=[py])
                y_, g_, bf = yn[cnt % 2], gs[cnt % 2], bufs[cnt % 2]
                cnt += 1
                gn_readout(P, bf, py, 256, 1, eps5, gain, bias, y_)
                for a in range(2):
                    K.tr(psg[:, a * 128:(a + 1) * 128], gT[:, a, qsl], P.ident[:], reads=[gT, P.ident], writes=[psg])
                K.op("act", lambda e: e.activation(g_[:], psg[:], AF.Silu), reads=[psg], writes=[g_])
                K.op("dve", lambda e: e.tensor_tensor(y_[:], y_[:], g_[:], ALU.mult), reads=[y_, g_], writes=[y_])
                K.dma("sp", ycat[qsl, 3072 + h * 256:3072 + (h + 1) * 256], y_[:], reads=[y_])
        K.barrier()


RW_NAMES = ["w0", "w1", "kk", "b0", "b1", "kd0", "kd1", "r", "v"]


def rwkv_scratch(P):
    P.scratch("zsA", [A_COLS, TT])
    for n in RW_NAMES:
        P.scratch("rw_" + n, [TT, 1024])
    P.scratch("rw_g", [TT, 1024])
    for d in range(2):
        P.scratch(f"rw_y{d}", [128, TT, 8])


def phase_rwkv_prep(P, l):
    K = P.K
    zT = P.scr["zT"]
    zsA = P.scr["zsA"]
    NCH = (A_COLS + 127) // 128
    with ExitStack() as st:
        mu = [K.sb(st, [128, NCH], name="rw_mu") for _ in range(2)]
        c0 = K.sb(st, [128, NCH], name="rw_c0")
        for i in range(2):
            K.op("pool", lambda e: e.memset(mu[i][:], 0.0), writes=[mu[i]])
            K.dma("sp", mu[i][:, 0:NCH - 1], P.inp["a_mu"][l, i, 0:(NCH - 1) * 128].rearrange("(c p) -> p c", p=128),
                  writes=[mu[i]], allow_slow_non_contiguous=True)
            K.dma("sp", mu[i][0:32, NCH - 1:NCH], P.inp["a_mu"][l, i, (NCH - 1) * 128:A_COLS].rearrange("(c p) -> p c", p=32),
                  writes=[mu[i]], allow_slow_non_contiguous=True)
        K.op("dve", lambda e: e.tensor_tensor(c0[:], mu[0][:], mu[1][:], ALU.add), reads=mu, writes=[c0])
        K.op("dve", lambda e: e.tensor_scalar(c0[:], c0[:], -1.0, 1.0, ALU.mult, ALU.add), reads=[c0], writes=[c0])
        zb = [K.sb(st, [128, TT], name="rw_z") for _ in range(2)]
        zo = [K.sb(st, [128, TT], name="rw_zs") for _ in range(2)]
        for c in range(NCH):
            pw = min(128, A_COLS - c * 128)
            z, o = zb[c % 2], zo[c % 2]
            K.dma("sp", z[:pw, :], zT[c * 128:c * 128 + pw, :], writes=[z])
            K.op("dve", lambda e: e.tensor_scalar(o[:pw, :], z[:pw, :], c0[:pw, c:c + 1], None, ALU.mult), reads=[z, c0], writes=[o])
            for (s0, s1) in ((0, CTX), (CTX, TT)):
                K.op("dve", lambda e: e.scalar_tensor_tensor(out=o[:pw, s0 + 1:s1], in0=z[:pw, s0:s1 - 1], scalar=mu[0][:pw, c:c + 1],
                                                             in1=o[:pw, s0 + 1:s1], op0=ALU.mult, op1=ALU.add), reads=[z, mu[0], o], writes=[o])
                K.op("dve", lambda e: e.scalar_tensor_tensor(out=o[:pw, s0:s1 - 1], in0=z[:pw, s0 + 1:s1], scalar=mu[1][:pw, c:c + 1],
                                                             in1=o[:pw, s0:s1 - 1], op0=ALU.mult, op1=ALU.add), reads=[z, mu[1], o], writes=[o])
            K.dma("pool", zsA[c * 128:c * 128 + pw, :], o[:pw, :], reads=[o])
        K.barrier()
    with ExitStack() as st:
        twc = [K.sb(st, [64, TT], name="rw_twc") for _ in range(2)]
        acin = [K.sb(st, [64, TT], name="rw_acin") for _ in range(2)]
        for d in range(2):
            K.dma("sp", twc[d][:], zsA[3072 + d * 64:3072 + (d + 1) * 64, :], writes=[twc[d]])
            K.op("act", lambda e: e.activation(twc[d][:], twc[d][:], AF.Tanh), reads=[twc[d]], writes=[twc[d]])
            K.dma("sp", acin[d][:], zsA[3200 + d * 64:3200 + (d + 1) * 64, :], writes=[acin[d]])
        bones = K.sb(st, [128, 128], name="rw_bones")
        K.op("pool", lambda e: e.memset(bones[:], 0.0), writes=[bones])
        K.op("pool", lambda e: e.memset(bones[0:64, 0:64], 1.0), writes=[bones])
        K.op("pool", lambda e: e.memset(bones[64:128, 64:128], 1.0), writes=[bones])
        eps12 = K.sb(st, [128, 1], name="rw_eps12")
        K.op("pool", lambda e: e.memset(eps12[:], 1e-12), writes=[eps12])
        X1, A0, A1, KT, KK, SQ = [K.sb(st, [128, TT], name=f"rw_b{i}") for i in range(6)]
        AD = [A0, A1]
        w2 = [K.sb(st, [64, 128], name="rw_w2") for _ in range(2)]
        a2 = [K.sb(st, [64, 128], name="rw_a2") for _ in range(2)]
        prm = K.sb(st, [128, 8], name="rw_prm")
        tmpb = K.sb(st, [128, 512], name="rw_tmpb")
        psm = [K.ps(st, [128, 512], name="rw_psm") for _ in range(2)]
        pst = [K.ps(st, [128, 4, 128], name="rw_pst") for _ in range(2)]
        tout = [K.sb(st, [128, 4, 128], name="rw_tout") for _ in range(2)]
        blocks = [(b0, min(512, TT - b0)) for b0 in range(0, TT, 512)]
        tcnt = [0]

        def emit_T(X, name, f0):
            dst = P.scr["rw_" + name]
            for g0 in range(0, NT, 4):
                gn = min(4, NT - g0)
                pt, to = pst[tcnt[0] % 2], tout[tcnt[0] % 2]
                tcnt[0] += 1
                for a in range(gn):
                    K.tr(pt[:, a, :], X[:, (g0 + a) * 128:(g0 + a + 1) * 128], P.ident[:], reads=[X, P.ident], writes=[pt])
                K.op("act", lambda e: e.copy(to[:, 0:gn, :], pt[:, 0:gn, :]), reads=[pt], writes=[to])
                K.dma("pool", dst[g0 * 128:(g0 + gn) * 128, f0:f0 + 128].rearrange("(a p) f -> p a f", p=128), to[:, 0:gn, :], reads=[to])

        for c in range(8):
            f0 = c * 128
            for d in range(2):
                K.dma("sp", w2[d][:], P.inp["a_w2"][l, d, :, f0:f0 + 128], writes=[w2[d]])
                K.dma("sp", a2[d][:], P.inp["a_a2"][l, d, :, f0:f0 + 128], writes=[a2[d]])
                K.dma("sp", prm[:, d:d + 1], P.inp["a_w0"][l, d, f0:f0 + 128].rearrange("(p o) -> p o", o=1), writes=[prm])
                K.dma("sp", prm[:, 2 + d:3 + d], P.inp["a_a0"][l, d, f0:f0 + 128].rearrange("(p o) -> p o", o=1), writes=[prm])
            K.dma("sp", prm[:, 4:5], P.inp["a_kk"][l, f0:f0 + 128].rearrange("(p o) -> p o", o=1), writes=[prm])
            K.dma("sp", prm[:, 5:6], P.inp["a_ka"][l, f0:f0 + 128].rearrange("(p o) -> p o", o=1), writes=[prm])
            K.op("dve", lambda e: e.tensor_scalar(prm[:, 6:7], prm[:, 5:6], -1.0, 1.0, ALU.mult, ALU.add), reads=[prm], writes=[prm])
            K.dma("sp", X1[:], zsA[f0:f0 + 128, :], writes=[X1])
            emit_T(X1, "r", f0)
            K.dma("sp", X1[:], zsA[2048 + f0:2048 + f0 + 128, :], writes=[X1])
            emit_T(X1, "v", f0)
            for d in range(2):
                for bi, (b0, bw) in enumerate(blocks):
                    ps = psm[bi % 2]
                    K.mm(ps[:, :bw], w2[d][:], twc[d][:, b0:b0 + bw], True, True, reads=[w2[d], twc[d]], writes=[ps])
                    K.op("act", lambda e: e.activation(tmpb[:, :bw], ps[:, :bw], AF.Sigmoid, bias=prm[:, d:d + 1], scale=1.0),
                         reads=[ps, prm], writes=[tmpb])
                    K.op("act", lambda e: e.activation(X1[:, b0:b0 + bw], tmpb[:, :bw], AF.Exp, scale=-0.6065306597), reads=[tmpb], writes=[X1])
                emit_T(X1, f"w{d}", f0)
            for d in range(2):
                for bi, (b0, bw) in enumerate(blocks):
                    ps = psm[bi % 2]
                    K.mm(ps[:, :bw], a2[d][:], acin[d][:, b0:b0 + bw], True, True, reads=[a2[d], acin[d]], writes=[ps])
                    K.op("act", lambda e: e.activation(AD[d][:, b0:b0 + bw], ps[:, :bw], AF.Sigmoid, bias=prm[:, 2 + d:3 + d], scale=1.0),
                         reads=[ps, prm], writes=[AD[d]])
            K.dma("sp", KT[:], zsA[1024 + f0:1024 + f0 + 128, :], writes=[KT])
            K.op("dve", lambda e: e.tensor_scalar(KK[:], KT[:], prm[:, 4:5], None, ALU.mult), reads=[KT, prm], writes=[KK])
            K.op("pool", lambda e: e.tensor_tensor(SQ[:], KK[:], KK[:], ALU.mult), reads=[KK], writes=[SQ])
            for bi, (b0, bw) in enumerate(blocks):
                ps = psm[bi % 2]
                K.mm(ps[:, :bw], bones[:], SQ[:, b0:b0 + bw], True, True, reads=[bones, SQ], writes=[ps])
                K.op("act", lambda e: e.activation(tmpb[:, :bw], ps[:, :bw], AF.Sqrt, bias=eps12[:, 0:1], scale=1.0), reads=[ps, eps12], writes=[tmpb])
                K.op("dve", lambda e: e.reciprocal(tmpb[:, :bw], tmpb[:, :bw]), reads=[tmpb], writes=[tmpb])
                K.op("dve", lambda e: e.tensor_tensor(KK[:, b0:b0 + bw], KK[:, b0:b0 + bw], tmpb[:, :bw], ALU.mult), reads=[KK, tmpb], writes=[KK])
            emit_T(KK, "kk", f0)
            for d in range(2):
                K.op("dve", lambda e: e.tensor_scalar(X1[:], AD[d][:], prm[:, 5:6], prm[:, 6:7], ALU.mult, ALU.add), reads=[AD[d], prm], writes=[X1])
                K.op("dve", lambda e: e.tensor_tensor(X1[:], X1[:], KT[:], ALU.mult), reads=[X1, KT], writes=[X1])
                emit_T(X1, f"kd{d}", f0)
                K.op("pool", lambda e: e.tensor_tensor(SQ[:], KK[:], AD[d][:], ALU.mult), reads=[KK, AD[d]], writes=[SQ])
                emit_T(SQ, f"b{d}", f0)
        K.barrier()
    with ExitStack() as st:
        ga = K.sb(st, [128, TT], name="rw_ga")
        gb = K.sb(st, [32, TT], name="rw_gb")
        g2a = K.sb(st, [128, 1024], name="rw_g2a")
        g2b = K.sb(st, [32, 1024], name="rw_g2b")
        K.dma("sp", ga[:], zsA[3328:3456, :], writes=[ga])
        K.dma("sp", gb[:], zsA[3456:3488, :], writes=[gb])
        K.dma("sp", g2a[:], P.inp["a_g2"][l, 0:128, :], writes=[g2a])
        K.dma("sp", g2b[:], P.inp["a_g2"][l, 128:160, :], writes=[g2b])
        K.op("act", lambda e: e.activation(ga[:], ga[:], AF.Sigmoid), reads=[ga], writes=[ga])
        K.op("act", lambda e: e.activation(gb[:], gb[:], AF.Sigmoid), reads=[gb], writes=[gb])
        pg = [K.ps(st, [128, 512], name="rw_pg") for _ in range(2)]
        og = [K.sb(st, [128, 1024], name="rw_og") for _ in range(2)]
        for ti in range(NT):
            o = og[ti % 2]
            for hb in range(2):
                p = pg[hb]
                K.mm(p[:], ga[:, ti * 128:(ti + 1) * 128], g2a[:, hb * 512:(hb + 1) * 512], True, False, reads=[ga, g2a], writes=[p])
                K.mm(p[:], gb[:, ti * 128:(ti + 1) * 128], g2b[:, hb * 512:(hb + 1) * 512], False, True, reads=[gb, g2b], writes=[p])
                K.op("act", lambda e: e.copy(o[:, hb * 512:(hb + 1) * 512], p[:]), reads=[p], writes=[o])
            K.dma("pool", P.scr["rw_g"][ti * 128:(ti + 1) * 128, :], o[:], reads=[o])
        K.barrier()


def phase_rwkv_scan(P, l, nsteps_dbg=None):
    K = P.K
    NS = 2
    NJ = 2
    JW = 8 // NJ
    with ExitStack() as st:
        S = [[K.sb(st, [128, JW, 64], name="rw_S") for _ in range(NJ)] for _ in range(2)]
        for d in range(2):
            for jh in range(NJ):
                K.op("pool", lambda e: e.memset(S[d][jh][:], 0.0), writes=[S[d][jh]])
        tmp = [[K.sb(st, [128, JW, 64], name="rw_tmp") for _ in range(NJ)] for _ in range(2)]
        vk = [[[K.sb(st, [128, JW, 64], name="rw_vk") for _ in range(2)] for _ in range(NJ)] for _ in range(2)]
        sa = [[K.sb(st, [128, JW], name="rw_sa") for _ in range(NJ)] for _ in range(2)]
        names = ["w", "kk", "b", "kd", "r"]
        bc = [[{n: K.sb(st, [128, 512], name="rw_bc") for n in names} for _ in range(2)] for _ in range(2)]
        rows = [[K.sb(st, [2, 5, NS, 512], name="rw_rows") for _ in range(2)] for _ in range(2)]
        sel2 = K.sb(st, [2, 128], name="rw_sel2")
        K.op("pool", lambda e: e.memset(sel2[:], 1.0), writes=[sel2])
        K.op("pool", lambda e: e.affine_select(out=sel2[:], in_=sel2[:], pattern=[[1, 128]], compare_op=ALU.is_ge, fill=0.0,
                                               base=0, channel_multiplier=-64), reads=[sel2], writes=[sel2])
        K.op("pool", lambda e: e.affine_select(out=sel2[:], in_=sel2[:], pattern=[[-1, 128]], compare_op=ALU.is_ge, fill=0.0,
                                               base=63, channel_multiplier=64), reads=[sel2], writes=[sel2])
        pbc = [K.ps(st, [128, 512], name="rw_pbc") for _ in range(6)]
        pcnt = [0]
        vP = [[K.sb(st, [128, 8, 128], name="rw_vP") for _ in range(2)] for _ in range(2)]
        yP = [[K.sb(st, [128, 128, 8], name="rw_yP") for _ in range(2)] for _ in range(2)]
        ydep = [[[Dep() for _ in range(NJ)] for _ in range(2)] for _ in range(2)]
        order = [list(range(NT)), [1, 0] + list(range(NT - 1, 1, -1))]
        zsA = P.scr["zsA"]
        nt_run = NT if nsteps_dbg is None else nsteps_dbg
        for n in range(nt_run):
            for d in range(2):
                ti = order[d][n]
                for c in range(2):
                    K.dma("pool", vP[d][n % 2][c * 64:(c + 1) * 64, :, :],
                          zsA[2048 + c * 512:2048 + (c + 1) * 512, ti * 128:(ti + 1) * 128].rearrange("(j v) t -> v j t", v=64),
                          writes=[vP[d][n % 2]])
            for q in range(128 // NS):
                for d in range(2):
                    ti = order[d][n]
                    if d == 0:
                        t0 = ti * 128 + q * NS
                    else:
                        t0 = ti * 128 + 128 - (q + 1) * NS
                    srcs = {"w": f"rw_w{d}", "kk": "rw_kk", "b": f"rw_b{d}", "kd": f"rw_kd{d}", "r": "rw_r"}
                    rw_ = rows[d][q % 2]
                    for i, nm in enumerate(names):
                        K.dma("sp" if i % 2 == 0 else "pool", rw_[:, i, :, :],
                              P.scr[srcs[nm]][t0:t0 + NS, :].rearrange("t (c n) -> c t n", c=2), writes=[rw_])
                for s in range(NS):
                    chains = []
                    for d in range(2):
                        si = s if d == 0 else NS - 1 - s
                        tau = (q * NS + s) if d == 0 else (127 - (q * NS + s))
                        buf = bc[d][s % 2]
                        rw_ = rows[d][q % 2]
                        for i, nm in enumerate(names):
                            pb = pbc[pcnt[0] % 6]
                            pcnt[0] += 1
                            K.mm(pb[:], sel2[:], rw_[:, i, si, :], True, True, reads=[sel2, rw_], writes=[pb])
                            K.op("act", lambda e: e.copy(buf[nm][:], pb[:]), reads=[pb], writes=[buf[nm]])
                        for jh in range(NJ):
                            chains.append((d, jh, buf, si, tau))

                    def v3(buf, nm, si, jh):
                        return buf[nm][:].rearrange("p (j k) -> p j k", k=64)[:, jh * JW:(jh + 1) * JW, :]

                    for (d, jh, buf, si, tau) in chains:
                        vkd = vk[d][jh][s % 2]
                        vsc = vP[d][n % 2][:, jh * JW:(jh + 1) * JW, tau:tau + 1].broadcast_to([128, JW, 64])
                        K.op("pool", lambda e: e.tensor_tensor(vkd[:], v3(buf, "kd", si, jh), vsc, ALU.mult),
                             reads=[buf["kd"], vP[d][n % 2]], writes=[vkd])
                    for (d, jh, buf, si, tau) in chains:
                        K.op("dve", lambda e: e.tensor_tensor(tmp[d][jh][:], S[d][jh][:], v3(buf, "kk", si, jh), ALU.mult),
                             reads=[S[d][jh], buf["kk"]], writes=[tmp[d][jh]])
                    for (d, jh, buf, si, tau) in chains:
                        K.op("dve", lambda e: e.tensor_reduce(sa[d][jh][:], tmp[d][jh][:], AX.X, ALU.add), reads=[tmp[d][jh]], writes=[sa[d][jh]])
                    for (d, jh, buf, si, tau) in chains:
                        K.op("dve", lambda e: e.tensor_tensor(S[d][jh][:], S[d][jh][:], v3(buf, "w", si, jh), ALU.mult),
                             reads=[S[d][jh], buf["w"]], writes=[S[d][jh]])
                    for (d, jh, buf, si, tau) in chains:
                        K.op("dve", lambda e: e.tensor_tensor(tmp[d][jh][:], v3(buf, "b", si, jh),
                                                              sa[d][jh][:, :].unsqueeze(2).broadcast_to([128, JW, 64]), ALU.mult),
                             reads=[buf["b"], sa[d][jh]], writes=[tmp[d][jh]])
                    for (d, jh, buf, si, tau) in chains:
                        K.op("dve", lambda e: e.tensor_tensor(S[d][jh][:], S[d][jh][:], tmp[d][jh][:], ALU.subtract),
                             reads=[S[d][jh], tmp[d][jh]], writes=[S[d][jh]])
                    for (d, jh, buf, si, tau) in chains:
                        vkd = vk[d][jh][s % 2]
                        K.op("dve", lambda e: e.tensor_tensor(S[d][jh][:], S[d][jh][:], vkd[:], ALU.add), reads=[S[d][jh], vkd], writes=[S[d][jh]])
                    for (d, jh, buf, si, tau) in chains:
                        K.op("dve", lambda e: e.tensor_tensor(tmp[d][jh][:], S[d][jh][:], v3(buf, "r", si, jh), ALU.mult),
                             reads=[S[d][jh], buf["r"]], writes=[tmp[d][jh]])
                    for (d, jh, buf, si, tau) in chains:
                        K.op("dve", lambda e: e.tensor_reduce(yP[d][n % 2][:, tau, jh * JW:(jh + 1) * JW], tmp[d][jh][:], AX.X, ALU.add),
                             reads=[tmp[d][jh]], writes=[ydep[d][n % 2][jh]])
            for d in range(2):
                ti = order[d][n]
                K.dma("sp", P.scr[f"rw_y{d}"][:, ti * 128:(ti + 1) * 128, :], yP[d][n % 2][:], reads=ydep[d][n % 2])
        K.barrier()


def phase_rwkv_readout(P, l, emit_ctx):
    K = P.K
    ycat = P.scr["ycat"]
    with ExitStack() as st:
        eps = K.sb(st, [128, 1], name="rw_epsgn")
        K.op("pool", lambda e: e.memset(eps[:], 64e-5), writes=[eps])
        gain = K.sb(st, [128, 1024], name="rw_gain")
        bias = K.sb(st, [128, 1024], name="rw_bias")
        rk = K.sb(st, [128, 1024], name="rw_rk")
        K.dma("sp", gain[:], P.inp["a_gn"][l, 0, :].partition_broadcast(128), writes=[gain])
        K.dma("sp", bias[:], P.inp["a_gn"][l, 1, :].partition_broadcast(128), writes=[bias])
        K.dma("sp", rk[:], P.inp["a_rk"][l].rearrange("h k -> (h k)").partition_broadcast(128), writes=[rk])
        ys = [K.sb(st, [128, 128, 8], name="rw_ys") for _ in range(2)]
        ytok = K.sb(st, [128, 1024], name="rw_ytok")
        yn = K.sb(st, [128, 1024], name="rw_yn")
        rt = K.sb(st, [128, 1024], name="rw_rt")
        kt = K.sb(st, [128, 1024], name="rw_kt")
        vt = K.sb(st, [128, 1024], name="rw_vt")
        gt = K.sb(st, [128, 1024], name="rw_gt")
        bs = K.sb(st, [128, 16], name="rw_bs")
        pt = [K.ps(st, [128, 128], name="rw_rpt") for _ in range(2)]
        bufs = (K.sb(st, [128, 16, 6]), K.sb(st, [128, 16, 2]), K.sb(st, [128, 16]))
        for ti in (list(range(NT)) if emit_ctx else list(range(2, NT))):
            tsl = slice(ti * 128, (ti + 1) * 128)
            for d in range(2):
                K.dma("sp", ys[d][:], P.scr[f"rw_y{d}"][:, tsl, :], writes=[ys[d]])
            K.dma("pool", rt[:], P.scr["rw_r"][tsl, :], writes=[rt])
            K.dma("pool", kt[:], P.scr["rw_kd0"][tsl, :], writes=[kt])
            K.dma("pool", vt[:], P.scr["rw_v"][tsl, :], writes=[vt])
            K.dma("pool", gt[:], P.scr["rw_g"][tsl, :], writes=[gt])
            K.op("dve", lambda e: e.tensor_tensor(ys[0][:], ys[0][:], ys[1][:], ALU.add), reads=ys, writes=[ys[0]])
            yt4 = ytok[:].rearrange("p (c j v) -> p c j v", c=2, j=8)
            for j in range(8):
                p = pt[j % 2]
                K.tr(p[:], ys[0][:, :, j], P.ident[:], reads=[ys[0], P.ident], writes=[p])
                K.op("act", lambda e: e.copy(yt4[:, :, j, :], p[:].rearrange("p (c v) -> p c v", c=2)), reads=[p], writes=[ytok])
            gn_readout(P, bufs, ytok, 64, 16, eps, gain, bias, yn)
            K.op("dve", lambda e: e.tensor_tensor(rt[:], rt[:], kt[:], ALU.mult), reads=[rt, kt], writes=[rt])
            K.op("dve", lambda e: e.tensor_tensor(rt[:], rt[:], rk[:], ALU.mult), reads=[rt, rk], writes=[rt])
            K.op("dve", lambda e: e.tensor_reduce(bs[:], rt[:].rearrange("p (h k) -> p h k", k=64), AX.X, ALU.add), reads=[rt], writes=[bs])
            K.op("dve", lambda e: e.tensor_tensor(vt[:].rearrange("p (h k) -> p h k", k=64), vt[:].rearrange("p (h k) -> p h k", k=64),
                                                  bs[:, :].unsqueeze(2).broadcast_to([128, 16, 64]), ALU.mult), reads=[vt, bs], writes=[vt])
            K.op("dve", lambda e: e.tensor_tensor(yn[:], yn[:], vt[:], ALU.add), reads=[yn, vt], writes=[yn])
            K.op("dve", lambda e: e.tensor_tensor(yn[:], yn[:], gt[:], ALU.mult), reads=[yn, gt], writes=[yn])
            K.dma("sp", ycat[tsl, 0:1024], yn[:], reads=[yn])
        K.barrier()


def phase_transT(P, src, dst, tiles):
    K = P.K
    with ExitStack() as st:
        xts = [K.sb(st, [128, D], name="tt_x") for _ in range(2)]
        hts = [K.sb(st, [128, KC, 128], name="tt_h") for _ in range(2)]
        pts = [K.ps(st, [128, 4, 128], name="tt_p") for _ in range(4)]
        for i, ti in enumerate(tiles):
            xt, ht = xts[i % 2], hts[i % 2]
            K.dma("sp", xt[:], src[ti * 128:(ti + 1) * 128, :], writes=[xt])
            for g in range(KC // 4):
                pt = pts[g % 4]
                for j in range(4):
                    kc = g * 4 + j
                    K.tr(pt[:, j, :], xt[:, kc * 128:(kc + 1) * 128], P.ident[:], reads=[xt, P.ident], writes=[pt])
                if g % 2 == 0:
                    K.op("act", lambda e: e.copy(ht[:, g * 4:g * 4 + 4, :], pt[:]), reads=[pt], writes=[ht])
                else:
                    K.op("dve", lambda e: e.tensor_copy(ht[:, g * 4:g * 4 + 4, :], pt[:]), reads=[pt], writes=[ht])
            K.dma("pool", dst[:, ti * 128:(ti + 1) * 128].rearrange("(kc p) t -> p kc t", p=128), ht[:], reads=[ht])
        K.barrier()


def gemm_tm(P, W, inT, out, Kd, N, tiles):
    K = P.K
    kcs = Kd // 128
    with ExitStack() as st:
        wb = K.sb(st, [128, kcs, 512], name="gt_w")
        its = [K.sb(st, [128, kcs, 128], name="gt_i") for _ in range(2)]
        pps = [K.ps(st, [128, 512], name="gt_p") for _ in range(2)]
        ots = [K.sb(st, [128, 512], name="gt_o") for _ in range(3)]
        cnt = 0
        for n0 in range(0, N, 512):
            K.dma("sp", wb[:], W[:, n0:n0 + 512].rearrange("(kc p) n -> p kc n", p=128), writes=[wb])
            for ti in tiles:
                it, p, o = its[cnt % 2], pps[cnt % 2], ots[cnt % 3]
                K.dma("pool", it[:], inT[:, ti * 128:(ti + 1) * 128].rearrange("(kc p) t -> p kc t", p=128), writes=[it])
                for kc in range(kcs):
                    K.mm(p[:], it[:, kc, :], wb[:, kc, :], kc == 0, kc == kcs - 1, reads=[it, wb], writes=[p])
                if cnt % 2 == 0:
                    K.op("act", lambda e: e.copy(o[:], p[:]), reads=[p], writes=[o])
                else:
                    K.op("dve", lambda e: e.tensor_copy(o[:], p[:]), reads=[p], writes=[o])
                K.dma("sp", out[ti * 128:(ti + 1) * 128, n0:n0 + 512], o[:], reads=[o])
                cnt += 1
        K.barrier()


def phase_postnorm(P, l, xsrc, ysrc, gchunk, lnp, dst, tiles, dst_off=0):
    K = P.K
    mods = P.scr[f"mods{l}"].rearrange("(r m) f -> r (m f)", r=2)
    with ExitStack() as st:
        G = [K.sb(st, [128, D], name="pn_g") for _ in range(2)]
        for r in range(2):
            K.dma("sp", G[r][:], mods[r, gchunk * D:(gchunk + 1) * D].partition_broadcast(128), writes=[G[r]])
        lg = K.sb(st, [128, D], name="pn_lg")
        lb = K.sb(st, [128, D], name="pn_lb")
        K.dma("sp", lg[:], lnp[l, 0, :].partition_broadcast(128), writes=[lg])
        K.dma("sp", lb[:], lnp[l, 1, :].partition_broadcast(128), writes=[lb])
        xts = [K.sb(st, [128, D], name="pn_x") for _ in range(2)]
        yts = [K.sb(st, [128, D], name="pn_y") for _ in range(2)]
        bufs = [(K.sb(st, [128, 8, 6]), K.sb(st, [128, 2]), K.sb(st, [128, 1])) for _ in range(2)]
        for i, ti in enumerate(tiles):
            xt, yt, bf = xts[i % 2], yts[i % 2], bufs[i % 2]
            r = 1 if ti < CTX // 128 else 0
            K.dma("sp", xt[:], xsrc[ti * 128:(ti + 1) * 128, :], writes=[xt])
            K.dma("pool", yt[:], ysrc[ti * 128:(ti + 1) * 128, :], writes=[yt])
            K.op("pool", lambda e: e.tensor_tensor(yt[:], yt[:], G[r][:], ALU.mult), reads=[yt, G[r]], writes=[yt])
            K.op("dve", lambda e: e.scalar_tensor_tensor(out=xt[:], in0=xt[:], scalar=float(DN_ALPHA), in1=yt[:], op0=ALU.mult, op1=ALU.add),
                 reads=[xt, yt], writes=[xt])
            P.ln_tile(bf, xt, P.eps6)
            K.op("pool", lambda e: e.tensor_tensor(xt[:], xt[:], lg[:], ALU.mult), reads=[xt, lg], writes=[xt])
            K.op("dve", lambda e: e.tensor_tensor(xt[:], xt[:], lb[:], ALU.add), reads=[xt, lb], writes=[xt])
            K.dma("sp", dst[ti * 128 - dst_off:(ti + 1) * 128 - dst_off, :], xt[:], reads=[xt])
        K.barrier()


def phase_router(P, l, tiles, st):
    K = P.K
    hT = P.scr["hT"]
    mg = K.sb(st, [128, NT, NE], name="mo_mg")
    with ExitStack() as s2:
        wr = K.sb(s2, [128, KC, NE], name="mo_wr")
        K.dma("sp", wr[:], P.inp["w_router"][l].rearrange("(kc p) e -> p kc e", p=128), writes=[wr])
        aff = K.sb(s2, [128, NT, NE], name="mo_aff")
        affT = K.sb(s2, [NE, TT], name="mo_affT")
        its = [K.sb(s2, [128, KC, 128], name="mo_it") for _ in range(2)]
        pl = [K.ps(s2, [128, NE], name="mo_pl") for _ in range(2)]
        ptt = [K.ps(s2, [NE, 128], name="mo_pt") for _ in range(2)]
        mx = K.sb(s2, [128, 1], name="mo_mx")
        sm = K.sb(s2, [128, 1], name="mo_sm")
        for i, ti in enumerate(tiles):
            it, p = its[i % 2], pl[i % 2]
            K.dma("sp", it[:], hT[:, ti * 128:(ti + 1) * 128].rearrange("(kc p) t -> p kc t", p=128), writes=[it])
            for kc in range(KC):
                K.mm(p[:], it[:, kc, :], wr[:, kc, :], kc == 0, kc == KC - 1, reads=[it, wr], writes=[p])
            K.op("dve", lambda e: e.tensor_reduce(mx[:], p[:], AX.X, ALU.max), reads=[p], writes=[mx])
            K.op("dve", lambda e: e.tensor_scalar(mx[:], mx[:], -1.0, None, ALU.mult), reads=[mx], writes=[mx])
            K.op("act", lambda e: e.activation(aff[:, ti, :], p[:], AF.Exp, bias=mx[:, 0:1], scale=1.0), reads=[p, mx], writes=[aff])
            K.op("dve", lambda e: e.tensor_reduce(sm[:], aff[:, ti, :], AX.X, ALU.add), reads=[aff], writes=[sm])
            K.op("dve", lambda e: e.reciprocal(sm[:], sm[:]), reads=[sm], writes=[sm])
            K.op("dve", lambda e: e.tensor_scalar(aff[:, ti, :], aff[:, ti, :], sm[:, 0:1], None, ALU.mult), reads=[aff, sm], writes=[aff])
            pt = ptt[i % 2]
            K.tr(pt[:], aff[:, ti, :], P.ident[:], reads=[aff, P.ident], writes=[pt])
            K.op("act", lambda e: e.copy(affT[:, ti * 128:(ti + 1) * 128], pt[:]), reads=[pt], writes=[affT])
        segs = []
        if 0 in tiles:
            segs.append((0, CTX, 2 * CTX // NE, 1))
        segs.append((CTX, TT, 2 * SEQ // NE, 0))
        work = K.sb(s2, [NE, SEQ], name="mo_work")
        m8 = K.sb(s2, [NE, 8], name="mo_m8")
        thr_d = P.scr["thr"]
        thb = K.sb(s2, [128, 2, NE], name="mo_thb")
        msk = K.sb(s2, [128, NE], name="mo_msk")
        for (c0, c1, kk, r) in segs:
            n = c1 - c0
            cur = affT[:, c0:c1]
            cur_t = affT
            for itn in range(kk // 8):
                K.op("dve", lambda e: e.max(out=m8[:], in_=cur), reads=[cur_t], writes=[m8])
                if itn < kk // 8 - 1:
                    K.op("dve", lambda e: e.match_replace(out=work[:, 0:n], in_to_replace=m8[:], in_values=cur, imm_value=-1.0),
                         reads=[m8, cur_t], writes=[work])
                    cur = work[:, 0:n]
                    cur_t = work
            K.dma("sp", thr_d[r, :].rearrange("(e o) -> e o", o=1), m8[:, 7:8], reads=[m8])
        K.barrier()
        for r in range(2):
            K.dma("sp", thb[:, r, :], thr_d[r, :].partition_broadcast(128), writes=[thb])
        for ti in tiles:
            r = 1 if ti < CTX // 128 else 0
            K.op("dve", lambda e: e.tensor_tensor(msk[:], aff[:, ti, :], thb[:, r, :], ALU.is_ge), reads=[aff, thb], writes=[msk])
            K.op("dve", lambda e: e.tensor_tensor(mg[:, ti, :], aff[:, ti, :], msk[:], ALU.mult), reads=[aff, msk], writes=[mg])
        K.barrier()
    return mg


def phase_moe(P, l, tiles, out):
    K = P.K
    hT = P.scr["hT"]
    with ExitStack() as st:
        mg = phase_router(P, l, tiles, st)
        hb = K.sb(st, [128, KC, 256], name="mo_hb")
        acc = K.sb(st, [128, 2, D], name="mo_acc")
        hid = K.sb(st, [128, 8, 256], name="mo_hid")
        w1c = [K.sb(st, [128, KC, 128], name="mo_w1") for _ in range(2)]
        w3c = [K.sb(st, [128, KC, 128], name="mo_w3") for _ in range(2)]
        w2c = [K.sb(st, [128, 8, 512], name="mo_w2") for _ in range(2)]
        sl = [K.sb(st, [128, 256], name="mo_sl") for _ in range(2)]
        p1 = [K.ps(st, [128, 256], name="mo_p1") for _ in range(2)]
        p3 = [K.ps(st, [128, 256], name="mo_p3") for _ in range(2)]
        p2 = [K.ps(st, [128, 512], name="mo_p2") for _ in range(2)]
        c1 = c2 = 0
        for b0 in range(0, len(tiles), 2):
            tb = tiles[b0:b0 + 2]
            t0 = tb[0] * 128
            K.dma("sp", hb[:], hT[:, t0:t0 + 256].rearrange("(kc p) t -> p kc t", p=128), writes=[hb])
            for e in range(NE):
                for fc in range(8):
                    wa, wc = w1c[c1 % 2], w3c[c1 % 2]
                    pa, pc, s_ = p1[c1 % 2], p3[c1 % 2], sl[c1 % 2]
                    c1 += 1
                    K.dma("sp", wa[:], P.inp["w_e1"][l, e, :, fc * 128:(fc + 1) * 128].rearrange("(kc p) f -> p kc f", p=128), writes=[wa])
                    K.dma("pool", wc[:], P.inp["w_e3"][l, e, :, fc * 128:(fc + 1) * 128].rearrange("(kc p) f -> p kc f", p=128), writes=[wc])
                    for kc in range(KC):
                        K.mm(pa[:], wa[:, kc, :], hb[:, kc, :], kc == 0, kc == KC - 1, reads=[wa, hb], writes=[pa])
                    for kc in range(KC):
                        K.mm(pc[:], wc[:, kc, :], hb[:, kc, :], kc == 0, kc == KC - 1, reads=[wc, hb], writes=[pc])
                    K.op("act", lambda e_: e_.activation(s_[:], pa[:], AF.Silu), reads=[pa], writes=[s_])
                    K.op("dve", lambda e_: e_.tensor_tensor(hid[:, fc, :], s_[:], pc[:], ALU.mult), reads=[s_, pc], writes=[hid])
                for db in range(8):
                    w2 = w2c[c2 % 2]
                    K.dma("sp" if db % 2 == 0 else "pool", w2[:],
                          P.inp["w_e2"][l, e, :, db * 512:(db + 1) * 512].rearrange("(fc p) d -> p fc d", p=128), writes=[w2])
                    for a in range(2):
                        p = p2[c2 % 2]
                        c2 += 1
                        for fc in range(8):
                            K.mm(p[:], hid[:, fc, a * 128:(a + 1) * 128], w2[:, fc, :], fc == 0, fc == 7, reads=[hid, w2], writes=[p])
                        gsc = mg[:, tb[a], e:e + 1]
                        if e == 0:
                            K.op("dve", lambda e_: e_.tensor_scalar(acc[:, a, db * 512:(db + 1) * 512], p[:], gsc, None, ALU.mult),
                                 reads=[p, mg], writes=[acc])
                        else:
                            K.op("dve", lambda e_: e_.scalar_tensor_tensor(out=acc[:, a, db * 512:(db + 1) * 512], in0=p[:], scalar=gsc,
                                                                           in1=acc[:, a, db * 512:(db + 1) * 512], op0=ALU.mult, op1=ALU.add),
                                 reads=[p, mg, acc], writes=[acc])
            for a in range(2):
                K.dma("sp", out[tb[a] * 128:(tb[a] + 1) * 128, :], acc[:, a, :], reads=[acc])
        K.barrier()


CONST_SPECS = [("natb", [DEPTH, 16, 64, 16, 64]), ("ropeC", [128, SEQ]), ("ropeS", [128, SEQ]),
               ("permT", [128, 128]), ("ret_tab", [4, 128, 63, 128])]


def declare_all(P, wshapes=None):
    P.ext_in("x", [SEQ, D])
    P.ext_in("ctx", [CTX, D])
    P.ext_in("cc", [D, 2])
    for name, shape in WEIGHT_SPECS:
        P.ext_in(name, (wshapes or {}).get(name, shape))
    for name, shape in CONST_SPECS:
        P.ext_in(name, shape)
    for l in range(DEPTH):
        P.scratch(f"mods{l}", [384, 128])
    for name in ("hT", "zT"):
        P.scratch(name, [D if name == "hT" else IN_COLS, TT])
    for name in ("xcur", "ycat", "yattn", "x1"):
        P.scratch(name, [TT, D])
    P.scratch("thr", [2, NE])
    rwkv_scratch(P)
    P.out = P.nc.dram_tensor("out", [SEQ, D], F32, kind="ExternalOutput").ap()


def layer_front(P, l):
    emit_ctx = l < DEPTH - 1
    P.phase_ada(l)
    P.phase_modT(l, P.scr["xcur"], P.scr["hT"], 0, 1, list(range(NT)))
    P.gemm_fm(P.inp["w_in"][l], P.scr["hT"], P.scr["zT"], D, IN_COLS, 0, TT)
    phase_natten(P, l, emit_ctx)
    phase_retention(P, l, emit_ctx)
    phase_gla(P, l, emit_ctx)
    phase_rwkv_prep(P, l)
    phase_rwkv_scan(P, l)
    phase_rwkv_readout(P, l, emit_ctx)


def layer_back(P, l):
    emit_ctx = l < DEPTH - 1
    tiles = list(range(NT)) if emit_ctx else list(range(2, NT))
    phase_transT(P, P.scr["ycat"], P.scr["hT"], tiles)
    gemm_tm(P, P.inp["w_out"][l], P.scr["hT"], P.scr["yattn"], D, D, tiles)
    phase_postnorm(P, l, P.scr["xcur"], P.scr["yattn"], 2, P.inp["ln1"], P.scr["x1"], tiles)
    P.phase_modT(l, P.scr["x1"], P.scr["hT"], 3, 4, tiles)
    phase_moe(P, l, tiles, P.scr["yattn"])
    if emit_ctx:
        phase_postnorm(P, l, P.scr["x1"], P.scr["yattn"], 5, P.inp["ln2"], P.scr["xcur"], tiles)
    else:
        phase_postnorm(P, l, P.scr["x1"], P.scr["yattn"], 5, P.inp["ln2"], P.out, tiles, dst_off=CTX)


def build_full():
    P = Prog()
    K = P.K
    declare_all(P)
    with K.stack, ExitStack() as st:
        P.consts(st)
        K.dma("sp", P.scr["xcur"][0:CTX, :], P.inp["ctx"])
        K.dma("pool", P.scr["xcur"][CTX:TT, :], P.inp["x"])
        K.barrier()
        for l in range(DEPTH):
            layer_front(P, l)
            layer_back(P, l)
        K.barrier()
    return P


def kernel(**inputs):
    inputs = {k: np.asarray(v) for k, v in inputs.items()}
    B = inputs["x"].shape[0]
    P = build_full()
    C, S, permT = rope_tables()
    consts = {"natb": na_tables(inputs["c_rpb"].astype(np.float32)), "ropeC": C, "ropeS": S, "permT": permT, "ret_tab": ret_tables()}
    in_maps = []
    for b in range(B):
        m = {"x": np.ascontiguousarray(inputs["x"][b], dtype=np.float32),
             "ctx": np.ascontiguousarray(inputs["ctx"][b], dtype=np.float32),
             "cc": np.ascontiguousarray(np.stack([inputs["c"][b], inputs["c_ctx"]], axis=1), dtype=np.float32)}
        for name, _ in WEIGHT_SPECS:
            m[name] = np.ascontiguousarray(inputs[name], dtype=np.float32)
        m.update(consts)
        in_maps.append(m)
    res = run_bass_kernel_spmd(P.nc, in_maps, core_ids=list(range(B)))
    return np.stack([np.asarray(res.results[b]["out"], dtype=np.float32) for b in range(B)], axis=0)
```
